# Optimizing a Trainium2 kernel written in Bass

```python
import jax, jax.numpy as jnp
from jax import lax
import numpy as np

D_MODEL = 2048
BATCH = 4
SEQ = 4096
DEPTH = 1
DEC_BATCH = 32
DEC_SEQ = 64
PAST_LEN = 1024

CHUNK = 64
SGU_CHUNK = 128
A_HEADS = 8
A_HEAD_DIM = 128
A_WIDTH = A_HEADS * A_HEAD_DIM
POOL_WINDOWS = (2, 4, 8, 16)
B_GROUPS = len(POOL_WINDOWS)
B_GROUP_DIM = 256
B_WIDTH = B_GROUPS * B_GROUP_DIM
POOL_MAX = max(POOL_WINDOWS)
MIX_WIDTH = A_WIDTH + B_WIDTH
D_FF = 5632
EPS = 1e-6

kernel_name = "hybrid_sgu_pool_streaming_step"


def rmsnorm(x, g):
    xf = x.astype(jnp.float32)
    r = lax.rsqrt(jnp.mean(xf * xf, axis=-1, keepdims=True) + EPS)
    return (xf * r).astype(x.dtype) * g


def swiglu(h, w_gate, w_up, w_down):
    return (jax.nn.silu(h @ w_gate) * (h @ w_up)) @ w_down


def sgu_spatial(vn, w_s, b_s):
    bsz, L, H, hd = vn.shape
    lc = min(L, SGU_CHUNK)
    nc = L // lc
    mask = jnp.tril(jnp.ones((lc, lc), dtype=bool))
    wm = jnp.where(mask[None], w_s[:, :lc, :lc], jnp.zeros((), w_s.dtype))
    vc = vn.reshape(bsz, nc, lc, H, hd)
    s = jnp.einsum('hts,bcshd->bcthd', wm, vc) + jnp.transpose(b_s[:, :lc])[None, None, :, :, None]
    return s.reshape(bsz, L, H, hd)


def pool_mixer(xb, prefix, pos0, pool_w, pool_scale):
    bsz, L, C = xb.shape
    P = POOL_MAX - 1
    ext = jnp.concatenate([prefix.astype(xb.dtype), xb], axis=1)
    new_state = ext[:, -P:]
    extf = ext.astype(jnp.float32)
    csum = jnp.pad(jnp.cumsum(extf, axis=1), ((0, 0), (1, 0), (0, 0)))
    end = csum[:, P + 1:]
    pos = pos0 + jnp.arange(L)
    outs = []
    for g, w in enumerate(POOL_WINDOWS):
        sl = slice(g * B_GROUP_DIM, (g + 1) * B_GROUP_DIM)
        cnt = jnp.minimum(w, pos + 1).astype(jnp.float32)[None, :, None]
        start = csum[:, P + 1 - w:P + 1 - w + L, sl]
        outs.append((end[..., sl] - start) / cnt - extf[:, P:, sl])
    d = jnp.stack(outs, axis=2).astype(xb.dtype)
    y = jnp.einsum('blgc,gcd->blgd', d, pool_w).reshape(bsz, L, C) * pool_scale
    return y, new_state


def layer(x, pool_prefix, pos0, ffn1_norm, ffn1_w_gate, ffn1_w_up, ffn1_w_down, mix_norm, w_in,
          sgu_norm, sgu_w, sgu_b, pool_w, pool_scale, w_out, ffn2_norm, ffn2_w_gate, ffn2_w_up, ffn2_w_down):
    bsz, L, _ = x.shape
    x = x + 0.5 * swiglu(rmsnorm(x, ffn1_norm), ffn1_w_gate, ffn1_w_up, ffn1_w_down)
    z = rmsnorm(x, mix_norm) @ w_in
    u = jax.nn.gelu(z[..., :A_WIDTH], approximate=False)
    v = jax.nn.gelu(z[..., A_WIDTH:2 * A_WIDTH], approximate=False)
    xb = z[..., 2 * A_WIDTH:]
    vn = rmsnorm(v.reshape(bsz, L, A_HEADS, A_HEAD_DIM), sgu_norm.reshape(A_HEADS, A_HEAD_DIM))
    a_out = u * sgu_spatial(vn, sgu_w, sgu_b).reshape(bsz, L, A_WIDTH)
    b_out, pool_state = pool_mixer(xb, pool_prefix, pos0, pool_w, pool_scale)
    x = x + jnp.concatenate([a_out, b_out], axis=-1) @ w_out
    x = x + 0.5 * swiglu(rmsnorm(x, ffn2_norm), ffn2_w_gate, ffn2_w_up, ffn2_w_down)
    return x, pool_state, vn.reshape(bsz, L, A_WIDTH)


def setup_inputs(seed: int = 0) -> dict:
    key = jax.random.key(seed)
    ks = jax.random.split(key, 24)
    n = jax.random.normal
    f32 = jnp.float32
    def gain(k, shape):
        return 1.0 + 0.02 * n(k, shape, f32)
    return {
        "x_prompt": n(ks[0], (BATCH, SEQ, D_MODEL), f32),
        "x_sample": n(ks[1], (DEC_BATCH, DEC_SEQ, D_MODEL), f32),
        "cache_pool": n(ks[2], (DEPTH, DEC_BATCH, POOL_MAX - 1, B_WIDTH), f32),
        "ffn1_norm": gain(ks[3], (DEPTH, D_MODEL)),
        "ffn1_w_gate": n(ks[4], (DEPTH, D_MODEL, D_FF), f32) * D_MODEL ** -0.5,
        "ffn1_w_up": n(ks[5], (DEPTH, D_MODEL, D_FF), f32) * D_MODEL ** -0.5,
        "ffn1_w_down": n(ks[6], (DEPTH, D_FF, D_MODEL), f32) * D_FF ** -0.5,
        "mix_norm": gain(ks[7], (DEPTH, D_MODEL)),
        "w_in": n(ks[8], (DEPTH, D_MODEL, 2 * A_WIDTH + B_WIDTH), f32) * D_MODEL ** -0.5,
        "sgu_norm": gain(ks[9], (DEPTH, A_WIDTH)),
        "sgu_w": n(ks[10], (DEPTH, A_HEADS, SGU_CHUNK, SGU_CHUNK), f32) * SGU_CHUNK ** -0.5,
        "sgu_b": gain(ks[11], (DEPTH, A_HEADS, SGU_CHUNK)),
        "pool_w": n(ks[12], (DEPTH, B_GROUPS, B_GROUP_DIM, B_GROUP_DIM), f32) * B_GROUP_DIM ** -0.5,
        "pool_scale": gain(ks[13], (DEPTH, B_WIDTH)),
        "w_out": n(ks[14], (DEPTH, MIX_WIDTH, D_MODEL), f32) * MIX_WIDTH ** -0.5,
        "ffn2_norm": gain(ks[15], (DEPTH, D_MODEL)),
        "ffn2_w_gate": n(ks[16], (DEPTH, D_MODEL, D_FF), f32) * D_MODEL ** -0.5,
        "ffn2_w_up": n(ks[17], (DEPTH, D_MODEL, D_FF), f32) * D_MODEL ** -0.5,
        "ffn2_w_down": n(ks[18], (DEPTH, D_FF, D_MODEL), f32) * D_FF ** -0.5,
        "final_norm": gain(ks[19], (D_MODEL,)),
    }


def reference(x_prompt, x_sample, cache_pool, ffn1_norm, ffn1_w_gate, ffn1_w_up, ffn1_w_down, mix_norm,
              w_in, sgu_norm, sgu_w, sgu_b, pool_w, pool_scale, w_out, ffn2_norm, ffn2_w_gate, ffn2_w_up,
              ffn2_w_down, final_norm):
    xp, xs = x_prompt, x_sample
    zero_prefix = jnp.zeros((xp.shape[0], POOL_MAX - 1, B_WIDTH), xp.dtype)
    pool_p_list, pool_s_list, v_s_list = [], [], []
    for l in range(DEPTH):
        w = (ffn1_norm[l], ffn1_w_gate[l], ffn1_w_up[l], ffn1_w_down[l], mix_norm[l], w_in[l],
             sgu_norm[l], sgu_w[l], sgu_b[l], pool_w[l], pool_scale[l], w_out[l],
             ffn2_norm[l], ffn2_w_gate[l], ffn2_w_up[l], ffn2_w_down[l])
        xp, pool_p, _ = layer(xp, zero_prefix, 0, *w)
        xs, pool_s, v_s = layer(xs, cache_pool[l], PAST_LEN, *w)
        pool_p_list.append(pool_p)
        pool_s_list.append(pool_s)
        v_s_list.append(v_s)
    y_prompt = rmsnorm(xp, final_norm)
    y_sample = rmsnorm(xs, final_norm)
    state_pool_prompt = jnp.stack(pool_p_list, axis=0)
    state_pool_sample = jnp.stack(pool_s_list, axis=0)
    state_sgu_v_sample = jnp.stack(v_s_list, axis=0)
    return (y_prompt, y_sample, state_pool_prompt, state_pool_sample, state_sgu_v_sample)
```

```python
import numpy as np
import concourse.bass as bass
import concourse.mybir as mybir
from concourse.bass_utils import run_bass_kernel_spmd

F32 = mybir.dt.float32
BF16 = mybir.dt.bfloat16
AF = mybir.ActivationFunctionType
ALU = mybir.AluOpType
AX = mybir.AxisListType

D = 2048
DFF = 5632
KC = 16
FC = 44
T = 768
HALO = 16
TP = T + HALO
EW = 848
NPASS = 3
NS = 6
EPS = 1e-6
NCORES = 8
_ACCUM = True
_CHAIN_ENG = "dve"
_V_CROSS = False
SUB = [(16, 384), (400, 384)]
SUBH = [(0, 400), (400, 384)]


class _Sem:
    _n = 0

    def __init__(self, nc, name):
        self.h = nc.alloc_semaphore(name)
        self.id = _Sem._n
        _Sem._n += 1
        self.total = 0


class _Eng:
    def __init__(self, nc, name, eng):
        self.eng = eng
        self.sem = _Sem(nc, "e_" + name)
        self.count = 0
        self.waited = {}


class Builder:
    def __init__(self):
        self.nc = bass.Bass("TRN2", target_bir_lowering=False)
        nc = self.nc
        self.E = {
            "pe": _Eng(nc, "pe", nc.tensor),
            "act": _Eng(nc, "act", nc.scalar),
            "dve": _Eng(nc, "dve", nc.vector),
            "pool": _Eng(nc, "pool", nc.gpsimd),
            "sp": _Eng(nc, "sp", nc.sync),
        }

    def wait(self, en, deps):
        E = self.E[en]
        best = {}
        for ev in deps:
            if ev is None:
                continue
            if isinstance(ev, list):
                for e2 in ev:
                    if e2 is not None:
                        s, v = e2
                        if best.get(s.id, (None, 0))[1] < v:
                            best[s.id] = (s, v)
                continue
            s, v = ev
            if best.get(s.id, (None, 0))[1] < v:
                best[s.id] = (s, v)
        for sid, (s, v) in best.items():
            if E.waited.get(sid, 0) < v:
                E.eng.wait_ge(s.h, v)
                E.waited[sid] = v

    def op(self, en, fn, deps=(), sig=True):
        self.wait(en, deps)
        E = self.E[en]
        ins = fn(E.eng)
        if sig:
            E.count += 1
            ins.then_inc(E.sem.h, 1)
            return (E.sem, E.count)
        return None

    def last_event(self, en):
        E = self.E[en]
        return (E.sem, E.count) if E.count > 0 else None

    def dma(self, en, out, in_, sem, deps=()):
        self.wait(en, deps)
        E = self.E[en]
        E.eng.dma_start(out=out, in_=in_).then_inc(sem.h, 16)
        sem.total += 16
        return (sem, sem.total)


def build_program(npass=NPASS):
    B = Builder()
    nc = B.nc

    def din(name, shape):
        return nc.dram_tensor(name, list(shape), F32, kind="ExternalInput").ap()

    def dout(name, shape):
        return nc.dram_tensor(name, list(shape), F32, kind="ExternalOutput").ap()

    x_fm = din("x_fm", [NPASS, 128, KC, TP])
    pool_pref = din("pool_pref", [128, 8, 4, 16])
    cst_d = din("cst", [128, 136])
    wT2_d = din("sgu_wT2", [128, 2, 8, 128])
    mask2_d = din("sgu_mask2", [128, 2, 128])
    b2_d = din("sgu_b2", [128, 2, 128])
    sel_d = din("sel", [128, 8, 128])
    snorm_d = din("sgu_norm", [1, 1024])
    poolw_d = din("pool_w", [4, 256, 256])
    Wg = [din("ffn1_w_gate", [D, DFF]), din("ffn2_w_gate", [D, DFF])]
    Wu = [din("ffn1_w_up", [D, DFF]), din("ffn2_w_up", [D, DFF])]
    Wd = [din("ffn1_w_down", [DFF, D]), din("ffn2_w_down", [DFF, D])]
    w_in = din("w_in", [D, 3072])
    w_out = din("w_out", [D, D])

    y_fm = dout("y_fm", [NPASS, 128, KC, T])
    sp_prompt_o = dout("sp_prompt", [128, 8, 16])
    sp_sample_o = dout("sp_sample", [128, 8, 4, 16])
    sv_sample_o = dout("sv_sample", [256, 1024])

    def sb(name, shape, dt):
        return nc.alloc_sbuf_tensor(name, list(shape), dt)

    x = sb("x", [128, KC, TP], F32)
    h = sb("h", [128, KC, TP], BF16)
    inter_t = sb("inter", [128, FC * TP], BF16)
    ring_slots = [sb(f"ring{i}", [128, KC, 128], BF16) for i in range(NS)]
    v_sb = [sb(f"v_sb{i}", [128, 512], F32) for i in range(3)]
    scr = sb("scr", [128, 512], BF16)
    vn_bf = [sb(f"vn_bf{i}", [128, 512], BF16) for i in range(3)]
    vn_f32 = [sb(f"vn_f32{i}", [128, 512], F32) for i in range(2)]
    rstd = sb("rstd", [128, TP], F32)
    sqtmp = [sb(f"sqtmp{i}", [128, 400], F32) for i in range(2)]
    silu_tmp = [sb(f"silu{i}", [128, 400], F32) for i in range(2)]
    cst = sb("cst_sb", [128, 136], F32)
    ones_f32 = sb("ones_f32", [128, 128], F32)
    wmT = sb("wmT", [128, 2, 8, 128], BF16)
    sel = sb("sel_sb", [128, 8, 128], BF16)
    b_sb = sb("b_sb", [128, 2, 2, 128], BF16)
    norm_bc = sb("norm_bc", [128, 1024], F32)
    poolw = sb("poolw", [128, 4, 2, 256], BF16)
    epsb = sb("epsb", [128, 1], F32)
    ss4 = sb("ss4", [128, 4], F32)
    r4 = sb("r4", [128, 4], F32)
    tmp16 = sb("tmp16", [128, 16], F32)
    xb_save = sb("xb_save", [128, 8, 16], F32)
    sp_prompt_sb = sb("sp_prompt_sb", [128, 8, 16], F32)
    sp_sample_sb = sb("sp_sample_sb", [128, 8, 4, 16], F32)

    inter = inter_t[:, :].rearrange("p (f t) -> p f t", t=TP)
    u_buf = inter_t[:, 0:8 * TP].rearrange("p (j t) -> p j t", t=TP)
    d_buf = inter_t[:, 8 * TP:16 * TP].rearrange("p (j t) -> p j t", t=TP)
    EXT0 = 16 * TP
    ext = inter_t[:, EXT0:EXT0 + 2 * 8 * EW].bitcast(F32).rearrange("p (j t) -> p j t", t=EW)
    TA0 = EXT0 + 2 * 8 * EW
    tmpA = inter_t[:, TA0:TA0 + 2 * EW].bitcast(F32)
    tmpB = inter_t[:, TA0 + 2 * EW:TA0 + 4 * EW].bitcast(F32)
    b_out = inter_t[:, EXT0:EXT0 + 8 * TP].rearrange("p (j t) -> p j t", t=TP)
    ystage = inter_t[:, 0:2 * KC * T].bitcast(F32).rearrange("p (k t) -> p k t", t=T)
    su_wT2 = inter_t[:, 0:4096].bitcast(F32).rearrange("p (v h t) -> p v h t", v=2, h=8)
    su_mask = inter_t[:, 4096:4096 + 512].bitcast(F32).rearrange("p (v t) -> p v t", v=2)
    su_b2 = inter_t[:, 4608:4608 + 512].bitcast(F32).rearrange("p (v t) -> p v t", v=2)
    su_bhi = inter_t[:, 5120:5120 + 512].bitcast(F32).rearrange("p (v t) -> p v t", v=2)

    gains = cst[:, 0:64]
    pool_scale = cst[:, 64:72]
    invcnt = cst[:, 72:136].rearrange("p (g t) -> p g t", t=16)

    banks = [nc.alloc_psum_tensor(f"bank{i}", [128, 512], F32) for i in range(8)]
    bank_free = [None] * 8
    bank_ctr = [0]

    bank_reserved = set()

    def get_bank(reserve=False):
        while (bank_ctr[0] % 8) in bank_reserved:
            bank_ctr[0] += 1
        i = bank_ctr[0] % 8
        bank_ctr[0] += 1
        if reserve:
            bank_reserved.add(i)
        return i, banks[i], bank_free[i]

    def free_bank(i, evs):
        bank_free[i] = evs if isinstance(evs, list) else [evs]

    s_setup = _Sem(nc, "d_setup")
    s_setup_g = _Sem(nc, "d_setup_g")
    s_x = [[_Sem(nc, f"d_x{p}_{q}") for q in range(4)] for p in range(NPASS)]
    s_pref = _Sem(nc, "d_pref")
    s_out = _Sem(nc, "d_out")
    s_y = [_Sem(nc, f"d_y{p}") for p in range(NPASS)]
    s_sv = [_Sem(nc, "d_sv0"), _Sem(nc, "d_sv1")]
    ring_sem = [_Sem(nc, f"d_ring{i}") for i in range(NS)]

    wq = []
    ring_state = {"issued": 0, "acquired": 0, "released": 0}
    ring_last_use = [None] * NS

    def ring_issue():
        while ring_state["issued"] < min(len(wq), ring_state["released"] + NS):
            i = ring_state["issued"]
            s = i % NS
            src, nk = wq[i]
            B.dma("pool", ring_slots[s][:, 0:nk, :], src, ring_sem[s], deps=[ring_last_use[s]])
            ring_state["issued"] += 1

    def ring_acquire():
        i = ring_state["acquired"]
        ring_state["acquired"] += 1
        assert i < ring_state["issued"], "ring block not issued yet"
        s = i % NS
        return i, ring_slots[s], (ring_sem[s], 16 * (i // NS + 1))

    def ring_release(i, ev):
        assert i == ring_state["released"]
        ring_last_use[i % NS] = ev
        ring_state["released"] += 1
        ring_issue()

    def colblk(W, c0, r0=0, nk=KC):
        return (W[r0:r0 + nk * 128, c0:c0 + 128].rearrange("(k p) c -> p k c", p=128), nk)

    for p in range(npass):
        for wi in range(2):
            if wi == 1:
                for j in range(8):
                    wq.append(colblk(w_in, 2048 + j * 128))
                for j in range(8):
                    wq.append(colblk(w_in, j * 128))
                for j in range(8):
                    wq.append(colblk(w_in, 1024 + j * 128))
                for j in range(16):
                    wq.append(colblk(w_out, j * 128))
            for f in range(FC):
                wq.append(colblk(Wg[wi], f * 128))
                wq.append(colblk(Wu[wi], f * 128))
            for dch in range(KC):
                wq.append(colblk(Wd[wi], dch * 128, 0, 16))
                wq.append(colblk(Wd[wi], dch * 128, 2048, 16))
                wq.append(colblk(Wd[wi], dch * 128, 4096, 12))

    x_war = {q: None for q in range(4)}

    def load_x(p):
        c_lo = 0 if p == 0 else 16
        return [B.dma("sp", x[:, 4 * q:4 * q + 4, c_lo:TP], x_fm[p, :, 4 * q:4 * q + 4, c_lo:TP],
                      s_x[p][q], deps=[x_war[q]]) for q in range(4)]

    x_loaded = load_x(0)
    ev_cst = B.dma("sp", cst[:], cst_d, s_setup)
    B.dma("sp", su_wT2, wT2_d, s_setup)
    B.dma("sp", su_mask, mask2_d, s_setup)
    B.dma("sp", su_b2, b2_d, s_setup)
    ev_setup = B.dma("sp", norm_bc[:], bass.AP(snorm_d.tensor, 0, [[0, 128], [1, 1024]]), s_setup)
    ev_cst = ev_setup
    B.dma("pool", sel[:], sel_d, s_setup_g)
    ev_setup_g = B.dma("pool", poolw[:], poolw_d.rearrange("g (cc p) d -> p g cc d", p=128), s_setup_g)
    ring_issue()

    ev_ones = B.op("dve", lambda e: e.memset(ones_f32[:], 1.0 / D))
    ev_eps = B.op("dve", lambda e: e.memset(epsb[:], EPS))
    ev_wm = None
    for v in range(2):
        ev_wm = B.op("dve", lambda e, v=v: e.tensor_tensor(
            out=wmT[:, v, :, :], in0=su_wT2[:, v, :, :],
            in1=su_mask[:, v, :].unsqueeze(1).to_broadcast([128, 8, 128]), op=ALU.mult),
            deps=[ev_setup])
    ev_bhi = B.op("dve", lambda e: e.tensor_copy(out=b_sb[:, :, 0, :], in_=su_b2), deps=[ev_setup])
    ev_bhi2 = B.op("dve", lambda e: e.tensor_copy(out=su_bhi, in_=b_sb[:, :, 0, :]), deps=[ev_bhi])
    ev_blo = B.op("dve", lambda e: e.tensor_tensor(out=b_sb[:, :, 1, :], in0=su_b2, in1=su_bhi, op=ALU.subtract),
                  deps=[ev_bhi2])
    setup_done = [ev_ones, ev_eps, ev_wm, ev_blo, ev_setup, ev_setup_g, ev_cst]

    state = {"inter_war": [ev_blo, ev_wm]}

    sqst = {"free": [None, None], "cnt": 0}

    class StatsAcc:
        def __init__(self, subs):
            self.subs = subs
            self.n = [0] * len(subs)
            self.last = [None] * len(subs)
            self.pending = []

        def add(self, si, k, x_ev):
            c0, n = self.subs[si]
            if self.n[si] == 0:
                self.last[si] = B.op("act", lambda e: e.activation(
                    out=rstd[:, c0:c0 + n], in_=x[:, k, c0:c0 + n], func=AF.Square),
                    deps=[x_ev, state.get("rstd_readers")])
            else:
                b = sqst["cnt"] % 2
                sqst["cnt"] += 1
                assert all(pb != b for (_, pb, _) in self.pending)
                ev_sq = B.op("act", lambda e: e.activation(
                    out=sqtmp[b][:, 0:n], in_=x[:, k, c0:c0 + n], func=AF.Square),
                    deps=[x_ev, sqst["free"][b]])
                self.pending.append((si, b, ev_sq))
            self.n[si] += 1

        def flush(self, keep_last=False):
            while self.pending:
                if keep_last and len(self.pending) <= len(self.subs) and all(c == KC for c in self.n):
                    break
                si, b, ev_sq = self.pending.pop(0)
                c0, n = self.subs[si]
                ev = B.op("dve", lambda e: e.tensor_tensor(
                    out=rstd[:, c0:c0 + n], in0=rstd[:, c0:c0 + n], in1=sqtmp[b][:, 0:n], op=ALU.add),
                    deps=[ev_sq, self.last[si]])
                sqst["free"][b] = ev
                self.last[si] = ev

        def finalize(self, si):
            assert self.n[si] == KC
            c0, n = self.subs[si]
            tail = [pnd for pnd in self.pending if pnd[0] == si]
            self.pending = [pnd for pnd in self.pending if pnd[0] != si]
            bi, ps, bfree = get_bank()
            ev_mm = B.op("pe", lambda e: e.matmul(
                ps[:, 0:n], lhsT=ones_f32[:], rhs=rstd[:, c0:c0 + n], start=True, stop=(not tail)),
                deps=[self.last[si], ev_ones] + (list(bfree) if bfree else []))
            for i, (_, b, ev_sq) in enumerate(tail):
                ev_mm = B.op("pe", lambda e: e.matmul(
                    ps[:, 0:n], lhsT=ones_f32[:], rhs=sqtmp[b][:, 0:n], start=False, stop=(i == len(tail) - 1)),
                    deps=[ev_sq])
                sqst["free"][b] = ev_mm
            return bi, ps, ev_mm

    def xdep(x_evs, k):
        return x_evs(k) if callable(x_evs) else x_evs

    def rmsnorm_fm(gidx, subs, x_evs, out_fn, extra_deps=(), k_major=False, acc=None):
        out = {}
        pe_last = B.last_event("pe")
        rec = []
        assert acc is None or acc.subs == subs
        for si, (c0, n) in enumerate(subs):
            if acc is not None:
                bi, ps, ev_mm = acc.finalize(si)
            else:
                bi, ps, bfree = get_bank()
                for k in range(KC):
                    b = sqst["cnt"] % 2
                    sqst["cnt"] += 1
                    ev_sq = B.op("act", lambda e: e.activation(
                        out=sqtmp[b][:, 0:n], in_=x[:, k, c0:c0 + n], func=AF.Square),
                        deps=[xdep(x_evs, k), sqst["free"][b]])
                    ev_mm = B.op("pe", lambda e: e.matmul(
                        ps[:, 0:n], lhsT=ones_f32[:], rhs=sqtmp[b][:, 0:n], start=(k == 0), stop=(k == KC - 1)),
                        deps=[ev_sq, ev_ones] + (list(bfree) if (k == 0 and bfree) else []))
                    sqst["free"][b] = ev_mm
            ev_s = B.op("act", lambda e: e.activation(
                out=rstd[:, c0:c0 + n], in_=ps[:, 0:n], func=AF.Sqrt, bias=epsb[:, 0:1], scale=1.0),
                deps=[ev_mm, ev_eps, state.get("rstd_readers")])
            free_bank(bi, ev_s)
            ev_r = B.op("dve", lambda e: e.reciprocal(out=rstd[:, c0:c0 + n], in_=rstd[:, c0:c0 + n]), deps=[ev_s])
            rec.append(ev_r)

            def emit_out(si, k, c0=c0, n=n, ev_r=ev_r):
                en = "dve"
                out[(si, k)] = B.op(en, lambda e: e.scalar_tensor_tensor(
                    out=out_fn(k, c0, n), in0=x[:, k, c0:c0 + n],
                    scalar=gains[:, gidx * 16 + k:gidx * 16 + k + 1], in1=rstd[:, c0:c0 + n],
                    op0=ALU.mult, op1=ALU.mult),
                    deps=[ev_r, xdep(x_evs, k), ev_cst, pe_last] + list(extra_deps))
            if not k_major:
                for k in range(KC):
                    emit_out(si, k)
            else:
                rec[-1] = (emit_out, si)
        if k_major:
            for k in range(KC):
                for (fn, si) in rec:
                    fn(si, k)
        state["rstd_readers"] = list(out.values())
        return out

    def ffn(wi, gidx, subs, x_evs, acc_in=None):
        h_ev = rmsnorm_fm(gidx, subs, x_evs, lambda k, c0, n: h[:, k, c0:c0 + n], acc=acc_in)
        x_all = x_evs("all") if callable(x_evs) else x_evs
        inter_evs = []
        silu_free = [None, None]
        step = 0
        first_write_deps = state["inter_war"]
        for f in range(FC):
            ig, slot_g, rdy_g = ring_acquire()
            iu, slot_u, rdy_u = ring_acquire()
            last_pe = None
            for si, (c0, n) in enumerate(subs):
                bg, psg, fg = get_bank()
                bu, psu, fu = get_bank()
                for k in range(KC):
                    dk = ([rdy_g] + (list(fg) if fg else [])) if k == 0 else []
                    if f == 0:
                        dk.append(h_ev[(si, k)])
                    ev_g = B.op("pe", lambda e, k=k: e.matmul(
                        psg[:, 0:n], lhsT=slot_g[:, k, :], rhs=h[:, k, c0:c0 + n],
                        start=(k == 0), stop=(k == KC - 1)),
                        deps=dk, sig=(k == KC - 1))
                for k in range(KC):
                    ev_u = B.op("pe", lambda e, k=k: e.matmul(
                        psu[:, 0:n], lhsT=slot_u[:, k, :], rhs=h[:, k, c0:c0 + n],
                        start=(k == 0), stop=(k == KC - 1)),
                        deps=([rdy_u] + (list(fu) if fu else [])) if k == 0 else (),
                        sig=(k == KC - 1))
                last_pe = ev_u
                sbi = step % 2
                ev_a = B.op("act", lambda e, sbi=sbi: e.activation(
                    out=silu_tmp[sbi][:, 0:n], in_=psg[:, 0:n], func=AF.Silu),
                    deps=[ev_g, silu_free[sbi]])
                free_bank(bg, ev_a)
                ev_i = B.op("dve", lambda e, sbi=sbi: e.tensor_tensor(
                    out=inter[:, f, c0:c0 + n], in0=silu_tmp[sbi][:, 0:n], in1=psu[:, 0:n], op=ALU.mult),
                    deps=[ev_a, ev_u] + list(first_write_deps))
                first_write_deps = []
                free_bank(bu, ev_i)
                silu_free[sbi] = ev_i
                inter_evs.append(ev_i)
                step += 1
            ring_release(ig, last_pe)
            ring_release(iu, last_pe)
        inter_done = inter_evs[-len(subs):]
        x_out = []
        acc = StatsAcc(subs)
        for dch in range(KC):
            blks = [ring_acquire() for _ in range(3)]
            last_pe = None
            for si, (c0, n) in enumerate(subs):
                bi, ps, bf_ = get_bank()
                cnt = 0
                for r, (ib, slot, rdy) in enumerate(blks):
                    nk = 16 if r < 2 else 12
                    for kk in range(nk):
                        f = r * 16 + kk
                        deps = []
                        if kk == 0:
                            deps.append(rdy)
                        if cnt == 0:
                            deps += inter_done + (list(bf_) if bf_ else [])
                        ev = B.op("pe", lambda e, slot=slot, kk=kk, f=f, cnt=cnt: e.matmul(
                            ps[:, 0:n], lhsT=slot[:, kk, :], rhs=inter[:, f, c0:c0 + n],
                            start=(cnt == 0), stop=(cnt == FC - 1)),
                            deps=deps, sig=(cnt == FC - 1))
                        cnt += 1
                last_pe = ev
                acc.flush()
                ev_x = B.op("dve", lambda e: e.scalar_tensor_tensor(
                    out=x[:, dch, c0:c0 + n], in0=ps[:, 0:n], scalar=0.5, in1=x[:, dch, c0:c0 + n],
                    op0=ALU.mult, op1=ALU.add), deps=[ev, x_all])
                free_bank(bi, ev_x)
                x_out.append(ev_x)
                acc.add(si, dch, ev_x)
            for (ib, slot, rdy) in blks:
                ring_release(ib, last_pe)
        acc.flush(keep_last=True)
        state["inter_war"] = [last_pe]
        return x_out, acc

    def mixer(p, subs_b, x_evs, acc_in=None):
        is_a = (p == 0)
        is_c = (p == 2)
        hm_ev = rmsnorm_fm(1, subs_b, x_evs, lambda k, c0, n: h[:, k, c0:c0 + n], acc=acc_in)
        hm_evs = list(hm_ev.values())
        war = state["inter_war"]
        pre_evs = []
        if p > 0:
            pre_evs.append(B.op("dve", lambda e: e.tensor_copy(out=ext[:, :, 0:16], in_=xb_save[:]),
                                deps=list(war) + hm_evs))
        if is_c:
            for q in range(4):
                ev_pf = B.dma("sp", ext[:, :, 528 + 80 * q:528 + 80 * q + 16], pool_pref[:, :, q, :], s_pref,
                              deps=list(war) + hm_evs)
            pre_evs.append(ev_pf)
        xb_evs = []
        for j in range(8):
            ib, slot, rdy = ring_acquire()
            for si, (c0, n) in enumerate(subs_b):
                bi, ps, bf_ = get_bank()
                for k in range(KC):
                    dk = ([rdy] + (list(bf_) if bf_ else [])) if k == 0 else []
                    if j == 0:
                        dk.append(hm_ev[(si, k)])
                    ev = B.op("pe", lambda e, k=k: e.matmul(
                        ps[:, 0:n], lhsT=slot[:, k, :], rhs=h[:, k, c0:c0 + n],
                        start=(k == 0), stop=(k == KC - 1)),
                        deps=dk, sig=(k == KC - 1))
                if is_c and c0 == 400:
                    ev1 = B.op("act", lambda e: e.activation(
                        out=ext[:, j, 400:528], in_=ps[:, 0:128], func=AF.Copy), deps=[ev] + list(war) + pre_evs)
                    dst = ext[:, j, 528:848].rearrange("p (q e) -> p q e", e=80)[:, :, 16:80]
                    src = ps[:, 128:384].rearrange("p (q e) -> p q e", e=64)
                    ev2 = B.op("act", lambda e: e.activation(out=dst, in_=src, func=AF.Copy), deps=[ev])
                    free_bank(bi, ev2)
                    xb_evs += [ev1, ev2]
                else:
                    ev1 = B.op("act", lambda e: e.activation(
                        out=ext[:, j, c0:c0 + n], in_=ps[:, 0:n], func=AF.Copy), deps=[ev] + list(war))
                    free_bank(bi, ev1)
                    xb_evs.append(ev1)
            ring_release(ib, ev)
        save_evs = []
        if not is_c:
            save_evs.append(B.op("dve", lambda e: e.tensor_copy(out=xb_save[:], in_=ext[:, :, 768:784]),
                                 deps=xb_evs + pre_evs))
        else:
            e1 = B.op("dve", lambda e: e.tensor_copy(out=sp_prompt_sb[:], in_=ext[:, :, 512:528]),
                      deps=xb_evs + pre_evs)
            src = ext[:, :, 528:848].rearrange("p j (q e) -> p j q e", e=80)[:, :, :, 64:80]
            e2 = B.op("dve", lambda e: e.tensor_copy(out=sp_sample_sb[:], in_=src), deps=xb_evs + pre_evs)
            B.dma("sp", sp_prompt_o, sp_prompt_sb[:], s_out, deps=[e1])
            B.dma("sp", sp_sample_o, sp_sample_sb[:], s_out, deps=[e2])
            save_evs += [e1, e2]
        W = EW if is_c else TP
        d_evs = []
        chain_in = xb_evs + pre_evs + list(war)
        for j in range(8):
            g = j // 2
            w = 2 << g
            e = B.op(_CHAIN_ENG, lambda e_: e_.tensor_tensor(
                out=tmpA[:, 1:W], in0=ext[:, j, 1:W], in1=ext[:, j, 0:W - 1], op=ALU.add),
                deps=chain_in + d_evs[-1:])
            S = tmpA
            if w >= 4:
                e = B.op(_CHAIN_ENG, lambda e_: e_.tensor_tensor(
                    out=tmpB[:, 3:W], in0=tmpA[:, 3:W], in1=tmpA[:, 1:W - 2], op=ALU.add), deps=[e])
                S = tmpB
            if w >= 8:
                e = B.op(_CHAIN_ENG, lambda e_: e_.tensor_tensor(
                    out=tmpA[:, 7:W], in0=tmpB[:, 7:W], in1=tmpB[:, 3:W - 4], op=ALU.add), deps=[e])
                S = tmpA
            if w >= 16:
                e = B.op(_CHAIN_ENG, lambda e_: e_.tensor_tensor(
                    out=tmpB[:, 15:W], in0=tmpA[:, 15:W], in1=tmpA[:, 7:W - 8], op=ALU.add), deps=[e])
                S = tmpB
            npr = 528 if is_c else TP
            ed = B.op("dve", lambda e_, S=S: e_.scalar_tensor_tensor(
                out=d_buf[:, j, 16:npr], in0=S[:, 16:npr], scalar=1.0 / w, in1=ext[:, j, 16:npr],
                op0=ALU.mult, op1=ALU.subtract), deps=[e])
            if is_c:
                Sv = S[:, 528:848].rearrange("p (q e) -> p q e", e=80)[:, :, 16:80]
                Ev = ext[:, j, 528:848].rearrange("p (q e) -> p q e", e=80)[:, :, 16:80]
                Dv = d_buf[:, j, 528:784].rearrange("p (q e) -> p q e", e=64)
                ed = B.op("dve", lambda e_, Sv=Sv, Ev=Ev, Dv=Dv: e_.scalar_tensor_tensor(
                    out=Dv, in0=Sv, scalar=1.0 / w, in1=Ev, op0=ALU.mult, op1=ALU.subtract), deps=[e])
            if is_a:
                e3 = B.op("dve", lambda e_, S=S: e_.tensor_tensor(
                    out=tmp16[:], in0=S[:, 16:32], in1=invcnt[:, g, :], op=ALU.mult), deps=[e, ev_cst])
                ed = B.op("dve", lambda e_: e_.tensor_tensor(
                    out=d_buf[:, j, 16:32], in0=tmp16[:], in1=ext[:, j, 16:32], op=ALU.subtract), deps=[e3, ed])
            d_evs.append(ed)
        u_evs = []
        for j in range(8):
            ib, slot, rdy = ring_acquire()
            for si, (c0, n) in enumerate(SUB):
                bi, ps, bf_ = get_bank()
                for k in range(KC):
                    dk = ([rdy] + (list(bf_) if bf_ else [])) if k == 0 else []
                    if j == 0:
                        dk.append(hm_ev[(si, k)])
                    ev = B.op("pe", lambda e, k=k: e.matmul(
                        ps[:, 0:n], lhsT=slot[:, k, :], rhs=h[:, k, c0:c0 + n],
                        start=(k == 0), stop=(k == KC - 1)),
                        deps=dk, sig=(k == KC - 1))
                ev_u = B.op("act", lambda e: e.activation(
                    out=u_buf[:, j, c0:c0 + n], in_=ps[:, 0:n], func=AF.Gelu), deps=[ev] + list(war))
                free_bank(bi, ev_u)
                u_evs.append(ev_u)
            ring_release(ib, ev)
        a_evs = []
        b_evs = []
        v_free = [None, None, None]
        vnb_free = [None, None, None]
        vnf_free = [None, None]
        bout_war = d_evs + save_evs
        pendq = []
        vstate = {"tcount": 0, "last_pe": None}

        def emit_v(hg, ti, blks):
            tc0 = 16 + 128 * ti
            sample = is_c and ti >= 4
            bsel = vstate["tcount"] % 3
            vstate["tcount"] += 1
            fsel = ti % 2
            bi, ps, bf_ = get_bank()
            first = True
            for c, (ib, slot, rdy) in enumerate(blks):
                for k in range(KC):
                    deps = []
                    if k == 0 and ti == 0:
                        deps.append(rdy)
                    if first:
                        deps += hm_evs + (list(bf_) if bf_ else [])
                        first = False
                    ev = B.op("pe", lambda e: e.matmul(
                        ps[:, c * 128:(c + 1) * 128], lhsT=h[:, k, tc0:tc0 + 128], rhs=slot[:, k, :],
                        start=(k == 0), stop=(k == KC - 1)),
                        deps=deps, sig=(c == 3 and k == KC - 1))
            vstate["last_pe"] = ev
            ev_v = B.op("act", lambda e: e.activation(
                out=v_sb[bsel][:], in_=ps[:, :], func=AF.Gelu), deps=[ev, v_free[bsel]])
            free_bank(bi, ev_v)
            for hh in range(4):
                e_sq = B.op("act", lambda e: e.activation(
                    out=scr[:, hh * 128:(hh + 1) * 128], in_=v_sb[bsel][:, hh * 128:(hh + 1) * 128],
                    func=AF.Square, accum_out=ss4[:, hh:hh + 1]), deps=[ev_v])
            e_rt = B.op("act", lambda e: e.activation(
                out=r4[:], in_=ss4[:], func=AF.Sqrt, bias=epsb[:, 0:1], scale=1.0 / 128),
                deps=[e_sq, ev_eps, state.get("r4_free")])
            e_rc = B.op("dve", lambda e: e.reciprocal(out=r4[:], in_=r4[:]), deps=[e_rt])
            nb0 = hg * 512
            if sample:
                for hh in range(4):
                    e7 = B.op("dve", lambda e: e.scalar_tensor_tensor(
                        out=vn_f32[fsel][:, hh * 128:(hh + 1) * 128],
                        in0=v_sb[bsel][:, hh * 128:(hh + 1) * 128], scalar=r4[:, hh:hh + 1],
                        in1=norm_bc[:, nb0 + hh * 128:nb0 + (hh + 1) * 128], op0=ALU.mult, op1=ALU.mult),
                        deps=[e_rc, vnf_free[fsel], ev_setup])
                row0 = (ti - 4) * 128
                vnf_free[fsel] = B.dma("sp", sv_sample_o[row0:row0 + 128, hg * 512:(hg + 1) * 512],
                                       vn_f32[fsel][:], s_sv[fsel], deps=[e7])
            for hh in range(4):
                e8 = B.op("dve", lambda e: e.scalar_tensor_tensor(
                    out=vn_bf[bsel][:, hh * 128:(hh + 1) * 128],
                    in0=v_sb[bsel][:, hh * 128:(hh + 1) * 128], scalar=r4[:, hh:hh + 1],
                    in1=norm_bc[:, nb0 + hh * 128:nb0 + (hh + 1) * 128], op0=ALU.mult, op1=ALU.mult),
                    deps=[e_rc, vnb_free[bsel], ev_setup])
            v_free[bsel] = e8
            state["r4_free"] = e8
            pendq.append((hg, bsel, e8, sample, tc0))

        def emit_sgu():
            hg, pb, pev, psample, ptc0 = pendq.pop(0)
            var = 1 if psample else 0
            bi2, ps2, bf2 = get_bank()
            for hh in range(4):
                hd = hg * 4 + hh
                deps = [pev, ev_wm, ev_blo, ev_setup_g]
                if hh == 0:
                    deps += (list(bf2) if bf2 else [])
                o = ps2[:, hh * 128:(hh + 1) * 128]
                B.op("pe", lambda e: e.matmul(
                    o, lhsT=vn_bf[pb][:, hh * 128:(hh + 1) * 128], rhs=wmT[:, var, hd, :],
                    start=True, stop=False), deps=deps, sig=False)
                B.op("pe", lambda e: e.matmul(
                    o, lhsT=sel[:, hd, :], rhs=b_sb[:, var, 0, :], start=False, stop=False), sig=False)
                ev_s = B.op("pe", lambda e: e.matmul(
                    o, lhsT=sel[:, hd, :], rhs=b_sb[:, var, 1, :], start=False, stop=True),
                    sig=(hh == 3))
            vnb_free[pb] = ev_s
            uv = u_buf[:, hg * 4:(hg + 1) * 4, ptc0:ptc0 + 128]
            ev_a = B.op("dve", lambda e: e.tensor_tensor(
                out=uv, in0=ps2[:, :].rearrange("p (h t) -> p h t", t=128), in1=uv, op=ALU.mult),
                deps=[ev_s] + u_evs)
            free_bank(bi2, ev_a)
            a_evs.append(ev_a)

        def emit_poolw(groups):
            for (g, dd, c0, n) in groups:
                bi, ps, bf_ = get_bank()
                for cc in range(2):
                    ev = B.op("pe", lambda e: e.matmul(
                        ps[:, 0:n], lhsT=poolw[:, g, cc, dd * 128:(dd + 1) * 128],
                        rhs=d_buf[:, 2 * g + cc, c0:c0 + n], start=(cc == 0), stop=(cc == 1)),
                        deps=(d_evs + [ev_setup_g] + (list(bf_) if bf_ else [])) if cc == 0 else (),
                        sig=(cc == 1))
                jj = 2 * g + dd
                ev_b = B.op("act", lambda e: e.activation(
                    out=b_out[:, jj, c0:c0 + n], in_=ps[:, 0:n], func=AF.Copy,
                    scale=pool_scale[:, jj:jj + 1]), deps=[ev, ev_cst] + bout_war)
                free_bank(bi, ev_b)
                b_evs.append(ev_b)

        pw_groups = [(g, dd, c0, n) for g in range(4) for dd in range(2) for (c0, n) in SUB]
        if _V_CROSS:
            vsteps = [(hg, ti) for hg in range(2) for ti in range(6)]
            blks = None
            for idx in range(len(vsteps) + 2):
                if idx < len(vsteps):
                    hg, ti = vsteps[idx]
                    if ti == 0:
                        blks = [ring_acquire() for _ in range(4)]
                    emit_v(hg, ti, blks)
                    if ti == 5:
                        for (ib, slot, rdy) in blks:
                            ring_release(ib, vstate["last_pe"])
                else:
                    half = len(pw_groups) // 2
                    emit_poolw(pw_groups[:half] if idx == len(vsteps) else pw_groups[half:])
                if idx >= 2:
                    emit_sgu()
        else:
            for hg in range(2):
                blks = [ring_acquire() for _ in range(4)]
                for ti in range(8):
                    if ti < 6:
                        emit_v(hg, ti, blks)
                        if ti == 5:
                            for (ib, slot, rdy) in blks:
                                ring_release(ib, vstate["last_pe"])
                    if ti >= 2:
                        emit_sgu()
            emit_poolw(pw_groups)
        assert not pendq
        x_out = []
        acc = StatsAcc(SUB)
        for dch in range(KC):
            ib, slot, rdy = ring_acquire()
            for si, (c0, n) in enumerate(SUB):
                bi, ps, bf_ = get_bank()
                for m in range(16):
                    rhs = u_buf[:, m, c0:c0 + n] if m < 8 else b_out[:, m - 8, c0:c0 + n]
                    deps = []
                    if m == 0:
                        deps = [rdy] + a_evs + (list(bf_) if bf_ else [])
                    if m == 8:
                        deps = b_evs
                    ev = B.op("pe", lambda e, m=m, rhs=rhs: e.matmul(
                        ps[:, 0:n], lhsT=slot[:, m, :], rhs=rhs, start=(m == 0), stop=(m == 15)),
                        deps=deps, sig=(m == 15))
                acc.flush()
                ev_x = B.op("dve", lambda e: e.tensor_tensor(
                    out=x[:, dch, c0:c0 + n], in0=ps[:, 0:n], in1=x[:, dch, c0:c0 + n], op=ALU.add),
                    deps=[ev, x_evs])
                free_bank(bi, ev_x)
                x_out.append(ev_x)
                acc.add(si, dch, ev_x)
            ring_release(ib, ev)
        acc.flush(keep_last=True)
        state["inter_war"] = [ev]
        return x_out, acc

    y_store_prev = None
    for p in range(npass):
        if p > 0:
            x_loaded = load_x(p)

        def x_evs(k, x_loaded=x_loaded):
            return list(x_loaded) if k == "all" else [x_loaded[k // 4]]

        if y_store_prev is not None:
            state["inter_war"] = list(state["inter_war"]) + [y_store_prev]
        subs1 = SUBH if p == 0 else SUB
        x1, acc1 = ffn(0, 0, subs1, x_evs)
        x2, acc2 = mixer(p, subs1, x1, acc1)
        x3, acc3 = ffn(1, 2, SUB, x2, acc2)
        y_ev = rmsnorm_fm(3, SUB, x3, lambda k, c0, n: ystage[:, k, c0 - 16:c0 - 16 + n],
                          extra_deps=state["inter_war"], k_major=True, acc=acc3)
        for q in range(4):
            x_war[q] = [y_ev[(si, k)] for si in range(2) for k in range(4 * q, 4 * q + 4)]
        for q in range(2):
            y_store_prev = B.dma("sp", y_fm[p, :, 8 * q:8 * q + 8, :], ystage[:, 8 * q:8 * q + 8, :], s_y[p],
                                 deps=[y_ev[(si, k)] for si in range(2) for k in range(8 * q, 8 * q + 8)])
    B.wait("sp", [(s_y[p], s_y[p].total) for p in range(npass)] +
           [(s_out, s_out.total), (s_sv[0], s_sv[0].total), (s_sv[1], s_sv[1].total)])
    assert ring_state["acquired"] == len(wq), (ring_state, len(wq))
    return nc


_PROGRAM = None


def _fm(tok):
    n, f = tok.shape
    return np.ascontiguousarray(tok.reshape(n, f // 128, 128).transpose(2, 1, 0))


def _tm(fm):
    q, k, n = fm.shape
    return fm.transpose(2, 1, 0).reshape(n, k * 128)


def kernel(x_prompt, x_sample, cache_pool, ffn1_norm, ffn1_w_gate, ffn1_w_up, ffn1_w_down, mix_norm,
           w_in, sgu_norm, sgu_w, sgu_b, pool_w, pool_scale, w_out, ffn2_norm, ffn2_w_gate, ffn2_w_up,
           ffn2_w_down, final_norm):
    global _PROGRAM
    if _PROGRAM is None:
        _PROGRAM = build_program()
    nc = _PROGRAM
    in_maps = _prep(x_prompt, x_sample, cache_pool, ffn1_norm, ffn1_w_gate, ffn1_w_up, ffn1_w_down, mix_norm,
                    w_in, sgu_norm, sgu_w, sgu_b, pool_w, pool_scale, w_out, ffn2_norm, ffn2_w_gate, ffn2_w_up,
                    ffn2_w_down, final_norm)
    res = run_bass_kernel_spmd(nc, in_maps, core_ids=list(range(NCORES)))
    return _post(res.results)


def _prep(x_prompt, x_sample, cache_pool, ffn1_norm, ffn1_w_gate, ffn1_w_up, ffn1_w_down, mix_norm,
          w_in, sgu_norm, sgu_w, sgu_b, pool_w, pool_scale, w_out, ffn2_norm, ffn2_w_gate, ffn2_w_up,
          ffn2_w_down, final_norm):
    f32 = np.float32
    a = lambda t: np.asarray(t, dtype=f32)
    x_prompt, x_sample, cache_pool = a(x_prompt), a(x_sample), a(cache_pool)

    shared = {
        "ffn1_w_gate": np.ascontiguousarray(a(ffn1_w_gate)[0]),
        "ffn1_w_up": np.ascontiguousarray(a(ffn1_w_up)[0]),
        "ffn1_w_down": np.ascontiguousarray(a(ffn1_w_down)[0]),
        "ffn2_w_gate": np.ascontiguousarray(a(ffn2_w_gate)[0]),
        "ffn2_w_up": np.ascontiguousarray(a(ffn2_w_up)[0]),
        "ffn2_w_down": np.ascontiguousarray(a(ffn2_w_down)[0]),
        "w_in": np.ascontiguousarray(a(w_in)[0]),
        "w_out": np.ascontiguousarray(a(w_out)[0]),
        "sgu_norm": np.ascontiguousarray(a(sgu_norm)[0].reshape(1, 1024)),
        "pool_w": np.ascontiguousarray(a(pool_w)[0]),
    }
    sw = a(sgu_w)[0]
    wT2 = np.zeros((128, 2, 8, 128), f32)
    wT2[:, 0] = sw.transpose(2, 0, 1)
    blk = sw[:, :64, :64].transpose(2, 0, 1)
    wT2[0:64, 1, :, 0:64] = blk
    wT2[64:128, 1, :, 64:128] = blk
    s_i = np.arange(128)[:, None]
    t_i = np.arange(128)[None, :]
    mask2 = np.zeros((128, 2, 128), f32)
    mask2[:, 0] = (s_i <= t_i)
    mask2[:, 1] = ((s_i // 64) == (t_i // 64)) & ((s_i % 64) <= (t_i % 64))
    sb_ = a(sgu_b)[0]
    b2 = np.zeros((128, 2, 128), f32)
    b2[0:8, 0] = sb_
    b2[0:8, 1, 0:64] = sb_[:, :64]
    b2[0:8, 1, 64:128] = sb_[:, :64]
    sel = np.zeros((128, 8, 128), f32)
    for hh in range(8):
        sel[hh, hh, :] = 1.0
    shared.update({"sgu_wT2": wT2, "sgu_mask2": mask2, "sgu_b2": b2, "sel": sel})
    gains = np.concatenate([a(g).reshape(16, 128).T for g in
                            (a(ffn1_norm)[0], a(mix_norm)[0], a(ffn2_norm)[0], a(final_norm))], axis=1)
    pscale = a(pool_scale)[0].reshape(8, 128).T

    in_maps = []
    for c in range(NCORES):
        b, hf = c // 2, c % 2
        base = hf * 2048
        xf = np.zeros((NPASS, 128, KC, TP), f32)
        tokA = np.zeros((TP, D), f32)
        if hf == 1:
            tokA[0:16] = x_prompt[b, base - 16:base]
        tokA[16:] = x_prompt[b, base:base + 768]
        xf[0] = _fm(tokA)
        xf[1, :, :, 16:] = _fm(x_prompt[b, base + 768:base + 1536])
        tokC = np.concatenate([x_prompt[b, base + 1536:base + 2048],
                               x_sample[4 * c:4 * c + 4].reshape(256, D)], axis=0)
        xf[2, :, :, 16:] = _fm(tokC)
        pref = np.zeros((128, 8, 4, 16), f32)
        for q in range(4):
            rows = np.zeros((16, 1024), f32)
            rows[1:] = cache_pool[0, 4 * c + q]
            pref[:, :, q, :] = _fm(rows)
        inv = np.zeros((128, 4, 16), f32)
        for g, w in enumerate((2, 4, 8, 16)):
            if hf == 0:
                inv[:, g, :] = 1.0 / np.minimum(w, np.arange(16) + 1)
            else:
                inv[:, g, :] = 1.0 / w
        cst = np.concatenate([gains, pscale, inv.reshape(128, 64)], axis=1).astype(f32)
        m = dict(shared)
        m.update({"x_fm": xf, "pool_pref": pref, "cst": np.ascontiguousarray(cst)})
        in_maps.append(m)
    return in_maps


def _post(results):
    f32 = np.float32
    y_prompt = np.zeros((4, 4096, D), f32)
    y_sample = np.zeros((32, 64, D), f32)
    sp_p = np.zeros((1, 4, 15, 1024), f32)
    sp_s = np.zeros((1, 32, 15, 1024), f32)
    sv_s = np.zeros((1, 32, 64, 1024), f32)
    for c in range(NCORES):
        r = results[c]
        if r is None:
            continue
        b, hf = c // 2, c % 2
        base = hf * 2048
        yf = r["y_fm"]
        y_prompt[b, base:base + 768] = _tm(yf[0])
        y_prompt[b, base + 768:base + 1536] = _tm(yf[1])
        tc = _tm(yf[2])
        y_prompt[b, base + 1536:base + 2048] = tc[:512]
        y_sample[4 * c:4 * c + 4] = tc[512:].reshape(4, 64, D)
        if hf == 1:
            sp_p[0, b] = _tm(r["sp_prompt"])[1:]
        for q in range(4):
            sp_s[0, 4 * c + q] = _tm(r["sp_sample"][:, :, q, :])[1:]
        sv_s[0, 4 * c:4 * c + 4] = r["sv_sample"].reshape(4, 64, 1024)
    return (y_prompt, y_sample, sp_p, sp_s, sv_s)
```

```python
import numpy as np
import concourse.bass as bass
import concourse.mybir as mybir
from concourse.bass_utils import run_bass_kernel_spmd

F32 = mybir.dt.float32
BF16 = mybir.dt.bfloat16
AF = mybir.ActivationFunctionType
ALU = mybir.AluOpType
AX = mybir.AxisListType

D = 2048
DFF = 5632
KC = 16
FC = 44
T = 768
HALO = 16
TP = T + HALO
EW = 848
NPASS = 3
NS = 6
EPS = 1e-6
NCORES = 8
_ACCUM = True
_CHAIN_ENG = "dve"
_V_CROSS = False
SUB = [(16, 384), (400, 384)]
SUBH = [(0, 400), (400, 384)]


class _Sem:
    _n = 0

    def __init__(self, nc, name):
        self.h = nc.alloc_semaphore(name)
        self.id = _Sem._n
        _Sem._n += 1
        self.total = 0


class _Eng:
    def __init__(self, nc, name, eng):
        self.eng = eng
        self.sem = _Sem(nc, "e_" + name)
        self.count = 0
        self.waited = {}


class Builder:
    def __init__(self):
        self.nc = bass.Bass("TRN2", target_bir_lowering=False)
        nc = self.nc
        self.E = {
            "pe": _Eng(nc, "pe", nc.tensor),
            "act": _Eng(nc, "act", nc.scalar),
            "dve": _Eng(nc, "dve", nc.vector),
            "pool": _Eng(nc, "pool", nc.gpsimd),
            "sp": _Eng(nc, "sp", nc.sync),
        }

    def wait(self, en, deps):
        E = self.E[en]
        best = {}
        for ev in deps:
            if ev is None:
                continue
            if isinstance(ev, list):
                for e2 in ev:
                    if e2 is not None:
                        s, v = e2
                        if best.get(s.id, (None, 0))[1] < v:
                            best[s.id] = (s, v)
                continue
            s, v = ev
            if best.get(s.id, (None, 0))[1] < v:
                best[s.id] = (s, v)
        for sid, (s, v) in best.items():
            if E.waited.get(sid, 0) < v:
                E.eng.wait_ge(s.h, v)
                E.waited[sid] = v

    def op(self, en, fn, deps=(), sig=True):
        self.wait(en, deps)
        E = self.E[en]
        ins = fn(E.eng)
        if sig:
            E.count += 1
            ins.then_inc(E.sem.h, 1)
            return (E.sem, E.count)
        return None

    def last_event(self, en):
        E = self.E[en]
        return (E.sem, E.count) if E.count > 0 else None

    def dma(self, en, out, in_, sem, deps=()):
        self.wait(en, deps)
        E = self.E[en]
        E.eng.dma_start(out=out, in_=in_).then_inc(sem.h, 16)
        sem.total += 16
        return (sem, sem.total)


def build_program(npass=NPASS):
    B = Builder()
    nc = B.nc

    def din(name, shape):
        return nc.dram_tensor(name, list(shape), F32, kind="ExternalInput").ap()

    def dout(name, shape):
        return nc.dram_tensor(name, list(shape), F32, kind="ExternalOutput").ap()

    x_fm = din("x_fm", [NPASS, 128, KC, TP])
    pool_pref = din("pool_pref", [128, 8, 4, 16])
    cst_d = din("cst", [128, 136])
    wT2_d = din("sgu_wT2", [128, 2, 8, 128])
    mask2_d = din("sgu_mask2", [128, 2, 128])
    b2_d = din("sgu_b2", [128, 2, 128])
    sel_d = din("sel", [128, 8, 128])
    snorm_d = din("sgu_norm", [1, 1024])
    poolw_d = din("pool_w", [4, 256, 256])
    Wg = [din("ffn1_w_gate", [D, DFF]), din("ffn2_w_gate", [D, DFF])]
    Wu = [din("ffn1_w_up", [D, DFF]), din("ffn2_w_up", [D, DFF])]
    Wd = [din("ffn1_w_down", [DFF, D]), din("ffn2_w_down", [DFF, D])]
    w_in = din("w_in", [D, 3072])
    w_out = din("w_out", [D, D])

    y_fm = dout("y_fm", [NPASS, 128, KC, T])
    sp_prompt_o = dout("sp_prompt", [128, 8, 16])
    sp_sample_o = dout("sp_sample", [128, 8, 4, 16])
    sv_sample_o = dout("sv_sample", [256, 1024])

    def sb(name, shape, dt):
        return nc.alloc_sbuf_tensor(name, list(shape), dt)

    x = sb("x", [128, KC, TP], F32)
    h = sb("h", [128, KC, TP], BF16)
    inter_t = sb("inter", [128, FC * TP], BF16)
    ring_slots = [sb(f"ring{i}", [128, KC, 128], BF16) for i in range(NS)]
    v_sb = [sb(f"v_sb{i}", [128, 512], F32) for i in range(3)]
    scr = sb("scr", [128, 512], BF16)
    vn_bf = [sb(f"vn_bf{i}", [128, 512], BF16) for i in range(3)]
    vn_f32 = [sb(f"vn_f32{i}", [128, 512], F32) for i in range(2)]
    rstd = sb("rstd", [128, TP], F32)
    sqtmp = [sb(f"sqtmp{i}", [128, 400], F32) for i in range(2)]
    silu_tmp = [sb(f"silu{i}", [128, 400], F32) for i in range(2)]
    cst = sb("cst_sb", [128, 136], F32)
    ones_f32 = sb("ones_f32", [128, 128], F32)
    wmT = sb("wmT", [128, 2, 8, 128], BF16)
    sel = sb("sel_sb", [128, 8, 128], BF16)
    b_sb = sb("b_sb", [128, 2, 2, 128], BF16)
    norm_bc = sb("norm_bc", [128, 1024], F32)
    poolw = sb("poolw", [128, 4, 2, 256], BF16)
    epsb = sb("epsb", [128, 1], F32)
    ss4 = sb("ss4", [128, 4], F32)
    r4 = sb("r4", [128, 4], F32)
    tmp16 = sb("tmp16", [128, 16], F32)
    xb_save = sb("xb_save", [128, 8, 16], F32)
    sp_prompt_sb = sb("sp_prompt_sb", [128, 8, 16], F32)
    sp_sample_sb = sb("sp_sample_sb", [128, 8, 4, 16], F32)

    inter = inter_t[:, :].rearrange("p (f t) -> p f t", t=TP)
    u_buf = inter_t[:, 0:8 * TP].rearrange("p (j t) -> p j t", t=TP)
    d_buf = inter_t[:, 8 * TP:16 * TP].rearrange("p (j t) -> p j t", t=TP)
    EXT0 = 16 * TP
    ext = inter_t[:, EXT0:EXT0 + 2 * 8 * EW].bitcast(F32).rearrange("p (j t) -> p j t", t=EW)
    TA0 = EXT0 + 2 * 8 * EW
    tmpA = inter_t[:, TA0:TA0 + 2 * EW].bitcast(F32)
    tmpB = inter_t[:, TA0 + 2 * EW:TA0 + 4 * EW].bitcast(F32)
    b_out = inter_t[:, EXT0:EXT0 + 8 * TP].rearrange("p (j t) -> p j t", t=TP)
    ystage = inter_t[:, 0:2 * KC * T].bitcast(F32).rearrange("p (k t) -> p k t", t=T)
    su_wT2 = inter_t[:, 0:4096].bitcast(F32).rearrange("p (v h t) -> p v h t", v=2, h=8)
    su_mask = inter_t[:, 4096:4096 + 512].bitcast(F32).rearrange("p (v t) -> p v t", v=2)
    su_b2 = inter_t[:, 4608:4608 + 512].bitcast(F32).rearrange("p (v t) -> p v t", v=2)
    su_bhi = inter_t[:, 5120:5120 + 512].bitcast(F32).rearrange("p (v t) -> p v t", v=2)

    hstage = h[:, :, :].rearrange("p k t -> p (k t)")[:, 0:2 * 8 * T].bitcast(F32).rearrange("p (k t) -> p k t", t=T)
    gains = cst[:, 0:64]
    pool_scale = cst[:, 64:72]
    invcnt = cst[:, 72:136].rearrange("p (g t) -> p g t", t=16)

    banks = [nc.alloc_psum_tensor(f"bank{i}", [128, 512], F32) for i in range(8)]
    bank_free = [None] * 8
    bank_ctr = [0]

    bank_reserved = set()

    def get_bank(reserve=False):
        while (bank_ctr[0] % 8) in bank_reserved:
            bank_ctr[0] += 1
        i = bank_ctr[0] % 8
        bank_ctr[0] += 1
        if reserve:
            bank_reserved.add(i)
        return i, banks[i], bank_free[i]

    def free_bank(i, evs):
        bank_free[i] = evs if isinstance(evs, list) else [evs]

    s_setup = _Sem(nc, "d_setup")
    s_setup_g = _Sem(nc, "d_setup_g")
    s_x = [[_Sem(nc, f"d_x{p}_{q}") for q in range(4)] for p in range(NPASS)]
    s_pref = _Sem(nc, "d_pref")
    s_xpre = [[_Sem(nc, f"d_xpre{p}_{q}") for q in range(2)] for p in range(NPASS)]
    s_out = _Sem(nc, "d_out")
    s_y = [_Sem(nc, f"d_y{p}") for p in range(NPASS)]
    s_sv = [_Sem(nc, "d_sv0"), _Sem(nc, "d_sv1")]
    ring_sem = [_Sem(nc, f"d_ring{i}") for i in range(NS)]

    wq = []
    ring_state = {"issued": 0, "acquired": 0, "released": 0}
    ring_last_use = [None] * NS

    def ring_issue():
        while ring_state["issued"] < min(len(wq), ring_state["released"] + NS):
            i = ring_state["issued"]
            s = i % NS
            src, nk = wq[i]
            B.dma("pool", ring_slots[s][:, 0:nk, :], src, ring_sem[s], deps=[ring_last_use[s]])
            ring_state["issued"] += 1

    def ring_acquire():
        i = ring_state["acquired"]
        ring_state["acquired"] += 1
        assert i < ring_state["issued"], "ring block not issued yet"
        s = i % NS
        return i, ring_slots[s], (ring_sem[s], 16 * (i // NS + 1))

    def ring_release(i, ev):
        assert i == ring_state["released"]
        ring_last_use[i % NS] = ev
        ring_state["released"] += 1
        ring_issue()

    def colblk(W, c0, r0=0, nk=KC):
        return (W[r0:r0 + nk * 128, c0:c0 + 128].rearrange("(k p) c -> p k c", p=128), nk)

    for p in range(npass):
        for wi in range(2):
            if wi == 1:
                for j in range(8):
                    wq.append(colblk(w_in, 2048 + j * 128))
                for j in range(8):
                    wq.append(colblk(w_in, j * 128))
                for j in range(8):
                    wq.append(colblk(w_in, 1024 + j * 128))
                for j in range(16):
                    wq.append(colblk(w_out, j * 128))
            for f in range(FC):
                wq.append(colblk(Wg[wi], f * 128))
                wq.append(colblk(Wu[wi], f * 128))
            for dch in range(KC):
                wq.append(colblk(Wd[wi], dch * 128, 0, 16))
                wq.append(colblk(Wd[wi], dch * 128, 2048, 16))
                wq.append(colblk(Wd[wi], dch * 128, 4096, 12))

    x_war = {q: None for q in range(4)}

    xpre = {}

    def prefetch_x(p, h_free_ev):
        xpre[p] = [B.dma("sp", hstage[:, 4 * q:4 * q + 4, :], x_fm[p, :, 4 * q:4 * q + 4, 16:TP],
                         s_xpre[p][q], deps=[h_free_ev]) for q in range(2)]

    def load_x(p):
        c_lo = 0 if p == 0 else 16
        evs = []
        for q in range(4):
            if p in xpre and q < 2:
                evs.append(B.op("act", lambda e: e.activation(
                    out=x[:, 4 * q:4 * q + 4, 16:TP], in_=hstage[:, 4 * q:4 * q + 4, :], func=AF.Copy),
                    deps=[x_war[q], xpre[p][q]]))
            else:
                evs.append(B.dma("sp", x[:, 4 * q:4 * q + 4, c_lo:TP], x_fm[p, :, 4 * q:4 * q + 4, c_lo:TP],
                                 s_x[p][q], deps=[x_war[q]]))
        if p in xpre:
            state["h_war"] = evs[0:2]
            state["sq_stage"] = list(xpre[p])
        return evs

    x_loaded = load_x(0)
    ev_cst = B.dma("sp", cst[:], cst_d, s_setup)
    B.dma("sp", su_wT2, wT2_d, s_setup)
    B.dma("sp", su_mask, mask2_d, s_setup)
    B.dma("sp", su_b2, b2_d, s_setup)
    ev_setup = B.dma("sp", norm_bc[:], bass.AP(snorm_d.tensor, 0, [[0, 128], [1, 1024]]), s_setup)
    ev_cst = ev_setup
    B.dma("pool", sel[:], sel_d, s_setup_g)
    ev_setup_g = B.dma("pool", poolw[:], poolw_d.rearrange("g (cc p) d -> p g cc d", p=128), s_setup_g)
    ring_issue()

    ev_ones = B.op("dve", lambda e: e.memset(ones_f32[:], 1.0 / D))
    ev_eps = B.op("dve", lambda e: e.memset(epsb[:], EPS))
    ev_wm = None
    for v in range(2):
        ev_wm = B.op("dve", lambda e, v=v: e.tensor_tensor(
            out=wmT[:, v, :, :], in0=su_wT2[:, v, :, :],
            in1=su_mask[:, v, :].unsqueeze(1).to_broadcast([128, 8, 128]), op=ALU.mult),
            deps=[ev_setup])
    ev_bhi = B.op("dve", lambda e: e.tensor_copy(out=b_sb[:, :, 0, :], in_=su_b2), deps=[ev_setup])
    ev_bhi2 = B.op("dve", lambda e: e.tensor_copy(out=su_bhi, in_=b_sb[:, :, 0, :]), deps=[ev_bhi])
    ev_blo = B.op("dve", lambda e: e.tensor_tensor(out=b_sb[:, :, 1, :], in0=su_b2, in1=su_bhi, op=ALU.subtract),
                  deps=[ev_bhi2])
    setup_done = [ev_ones, ev_eps, ev_wm, ev_blo, ev_setup, ev_setup_g, ev_cst]

    state = {"inter_war": [ev_blo, ev_wm]}

    sqst = {"free": [None, None], "cnt": 0}

    class StatsAcc:
        def __init__(self, subs):
            self.subs = subs
            self.n = [0] * len(subs)
            self.last = [None] * len(subs)
            self.pending = []

        def add(self, si, k, x_ev):
            c0, n = self.subs[si]
            if self.n[si] == 0:
                self.last[si] = B.op("act", lambda e: e.activation(
                    out=rstd[:, c0:c0 + n], in_=x[:, k, c0:c0 + n], func=AF.Square),
                    deps=[x_ev, state.get("rstd_readers")])
            else:
                b = sqst["cnt"] % 2
                sqst["cnt"] += 1
                assert all(pb != b for (_, pb, _) in self.pending)
                ev_sq = B.op("act", lambda e: e.activation(
                    out=sqtmp[b][:, 0:n], in_=x[:, k, c0:c0 + n], func=AF.Square),
                    deps=[x_ev, sqst["free"][b]])
                self.pending.append((si, b, ev_sq))
            self.n[si] += 1

        def flush(self, keep_last=False):
            while self.pending:
                if keep_last and len(self.pending) <= len(self.subs) and all(c == KC for c in self.n):
                    break
                si, b, ev_sq = self.pending.pop(0)
                c0, n = self.subs[si]
                ev = B.op("dve", lambda e: e.tensor_tensor(
                    out=rstd[:, c0:c0 + n], in0=rstd[:, c0:c0 + n], in1=sqtmp[b][:, 0:n], op=ALU.add),
                    deps=[ev_sq, self.last[si]])
                sqst["free"][b] = ev
                self.last[si] = ev

        def finalize(self, si):
            assert self.n[si] == KC
            c0, n = self.subs[si]
            tail = [pnd for pnd in self.pending if pnd[0] == si]
            self.pending = [pnd for pnd in self.pending if pnd[0] != si]
            bi, ps, bfree = get_bank()
            ev_mm = B.op("pe", lambda e: e.matmul(
                ps[:, 0:n], lhsT=ones_f32[:], rhs=rstd[:, c0:c0 + n], start=True, stop=(not tail)),
                deps=[self.last[si], ev_ones] + (list(bfree) if bfree else []))
            for i, (_, b, ev_sq) in enumerate(tail):
                ev_mm = B.op("pe", lambda e: e.matmul(
                    ps[:, 0:n], lhsT=ones_f32[:], rhs=sqtmp[b][:, 0:n], start=False, stop=(i == len(tail) - 1)),
                    deps=[ev_sq])
                sqst["free"][b] = ev_mm
            return bi, ps, ev_mm

    def xdep(x_evs, k):
        return x_evs(k) if callable(x_evs) else x_evs

    def rmsnorm_fm(gidx, subs, x_evs, out_fn, extra_deps=(), k_major=False, acc=None):
        out = {}
        pe_last = B.last_event("pe")
        rec = []
        assert acc is None or acc.subs == subs
        stage_sq = []
        for si, (c0, n) in enumerate(subs):
            if acc is not None:
                bi, ps, ev_mm = acc.finalize(si)
            else:
                bi, ps, bfree = get_bank()
                stage = state.get("sq_stage")
                for k in range(KC):
                    b = sqst["cnt"] % 2
                    sqst["cnt"] += 1
                    if stage is not None and k < 8 and si == 0:
                        src, sdep = hstage[:, k, c0 - 16:c0 - 16 + n], stage[k // 4]
                    else:
                        src, sdep = x[:, k, c0:c0 + n], xdep(x_evs, k)
                    ev_sq = B.op("act", lambda e: e.activation(
                        out=sqtmp[b][:, 0:n], in_=src, func=AF.Square),
                        deps=[sdep, sqst["free"][b]])
                    if stage is not None and k < 8 and si == 0:
                        stage_sq.append(ev_sq)
                    ev_mm = B.op("pe", lambda e: e.matmul(
                        ps[:, 0:n], lhsT=ones_f32[:], rhs=sqtmp[b][:, 0:n], start=(k == 0), stop=(k == KC - 1)),
                        deps=[ev_sq, ev_ones] + (list(bfree) if (k == 0 and bfree) else []))
                    sqst["free"][b] = ev_mm
                if si == len(subs) - 1:
                    state.pop("sq_stage", None)
            ev_s = B.op("act", lambda e: e.activation(
                out=rstd[:, c0:c0 + n], in_=ps[:, 0:n], func=AF.Sqrt, bias=epsb[:, 0:1], scale=1.0),
                deps=[ev_mm, ev_eps, state.get("rstd_readers")])
            free_bank(bi, ev_s)
            ev_r = B.op("dve", lambda e: e.reciprocal(out=rstd[:, c0:c0 + n], in_=rstd[:, c0:c0 + n]), deps=[ev_s])
            rec.append(ev_r)

            def emit_out(si, k, c0=c0, n=n, ev_r=ev_r):
                en = "dve"
                out[(si, k)] = B.op(en, lambda e: e.scalar_tensor_tensor(
                    out=out_fn(k, c0, n), in0=x[:, k, c0:c0 + n],
                    scalar=gains[:, gidx * 16 + k:gidx * 16 + k + 1], in1=rstd[:, c0:c0 + n],
                    op0=ALU.mult, op1=ALU.mult),
                    deps=[ev_r, xdep(x_evs, k), ev_cst, pe_last] + list(extra_deps) + stage_sq)
            if not k_major:
                for k in range(KC):
                    emit_out(si, k)
            else:
                rec[-1] = (emit_out, si)
        if k_major:
            for k in list(range(8, KC)) + list(range(8)):
                for (fn, si) in rec:
                    fn(si, k)
        state["rstd_readers"] = list(out.values())
        return out

    def ffn(wi, gidx, subs, x_evs, acc_in=None, after_gateup=None):
        h_ev = rmsnorm_fm(gidx, subs, x_evs, lambda k, c0, n: h[:, k, c0:c0 + n], acc=acc_in,
                          extra_deps=state.pop("h_war", []))
        x_all = x_evs("all") if callable(x_evs) else x_evs
        inter_evs = []
        silu_free = [None, None]
        step = 0
        first_write_deps = state["inter_war"]
        for f in range(FC):
            ig, slot_g, rdy_g = ring_acquire()
            iu, slot_u, rdy_u = ring_acquire()
            last_pe = None
            for si, (c0, n) in enumerate(subs):
                bg, psg, fg = get_bank()
                bu, psu, fu = get_bank()
                for k in range(KC):
                    dk = ([rdy_g] + (list(fg) if fg else [])) if k == 0 else []
                    if f == 0:
                        dk.append(h_ev[(si, k)])
                    ev_g = B.op("pe", lambda e, k=k: e.matmul(
                        psg[:, 0:n], lhsT=slot_g[:, k, :], rhs=h[:, k, c0:c0 + n],
                        start=(k == 0), stop=(k == KC - 1)),
                        deps=dk, sig=(k == KC - 1))
                for k in range(KC):
                    ev_u = B.op("pe", lambda e, k=k: e.matmul(
                        psu[:, 0:n], lhsT=slot_u[:, k, :], rhs=h[:, k, c0:c0 + n],
                        start=(k == 0), stop=(k == KC - 1)),
                        deps=([rdy_u] + (list(fu) if fu else [])) if k == 0 else (),
                        sig=(k == KC - 1))
                last_pe = ev_u
                sbi = step % 2
                ev_a = B.op("act", lambda e, sbi=sbi: e.activation(
                    out=silu_tmp[sbi][:, 0:n], in_=psg[:, 0:n], func=AF.Silu),
                    deps=[ev_g, silu_free[sbi]])
                free_bank(bg, ev_a)
                ev_i = B.op("dve", lambda e, sbi=sbi: e.tensor_tensor(
                    out=inter[:, f, c0:c0 + n], in0=silu_tmp[sbi][:, 0:n], in1=psu[:, 0:n], op=ALU.mult),
                    deps=[ev_a, ev_u] + list(first_write_deps))
                first_write_deps = []
                free_bank(bu, ev_i)
                silu_free[sbi] = ev_i
                inter_evs.append(ev_i)
                step += 1
            ring_release(ig, last_pe)
            ring_release(iu, last_pe)
        inter_done = inter_evs[-len(subs):]
        if after_gateup is not None:
            after_gateup(last_pe)
        x_out = []
        acc = StatsAcc(subs)
        for dch in range(KC):
            blks = [ring_acquire() for _ in range(3)]
            last_pe = None
            for si, (c0, n) in enumerate(subs):
                bi, ps, bf_ = get_bank()
                cnt = 0
                for r, (ib, slot, rdy) in enumerate(blks):
                    nk = 16 if r < 2 else 12
                    for kk in range(nk):
                        f = r * 16 + kk
                        deps = []
                        if kk == 0:
                            deps.append(rdy)
                        if cnt == 0:
                            deps += inter_done + (list(bf_) if bf_ else [])
                        ev = B.op("pe", lambda e, slot=slot, kk=kk, f=f, cnt=cnt: e.matmul(
                            ps[:, 0:n], lhsT=slot[:, kk, :], rhs=inter[:, f, c0:c0 + n],
                            start=(cnt == 0), stop=(cnt == FC - 1)),
                            deps=deps, sig=(cnt == FC - 1))
                        cnt += 1
                last_pe = ev
                acc.flush()
                ev_x = B.op("dve", lambda e: e.scalar_tensor_tensor(
                    out=x[:, dch, c0:c0 + n], in0=ps[:, 0:n], scalar=0.5, in1=x[:, dch, c0:c0 + n],
                    op0=ALU.mult, op1=ALU.add), deps=[ev, x_all])
                free_bank(bi, ev_x)
                x_out.append(ev_x)
                acc.add(si, dch, ev_x)
            for (ib, slot, rdy) in blks:
                ring_release(ib, last_pe)
        acc.flush(keep_last=True)
        state["inter_war"] = [last_pe]
        return x_out, acc

    def mixer(p, subs_b, x_evs, acc_in=None):
        is_a = (p == 0)
        is_c = (p == 2)
        hm_ev = rmsnorm_fm(1, subs_b, x_evs, lambda k, c0, n: h[:, k, c0:c0 + n], acc=acc_in)
        hm_evs = list(hm_ev.values())
        war = state["inter_war"]
        pre_evs = []
        if p > 0:
            pre_evs.append(B.op("dve", lambda e: e.tensor_copy(out=ext[:, :, 0:16], in_=xb_save[:]),
                                deps=list(war) + hm_evs))
        if is_c:
            for q in range(4):
                ev_pf = B.dma("sp", ext[:, :, 528 + 80 * q:528 + 80 * q + 16], pool_pref[:, :, q, :], s_pref,
                              deps=list(war) + hm_evs)
            pre_evs.append(ev_pf)
        xb_evs = []
        for j in range(8):
            ib, slot, rdy = ring_acquire()
            for si, (c0, n) in enumerate(subs_b):
                bi, ps, bf_ = get_bank()
                for k in range(KC):
                    dk = ([rdy] + (list(bf_) if bf_ else [])) if k == 0 else []
                    if j == 0:
                        dk.append(hm_ev[(si, k)])
                    ev = B.op("pe", lambda e, k=k: e.matmul(
                        ps[:, 0:n], lhsT=slot[:, k, :], rhs=h[:, k, c0:c0 + n],
                        start=(k == 0), stop=(k == KC - 1)),
                        deps=dk, sig=(k == KC - 1))
                if is_c and c0 == 400:
                    ev1 = B.op("act", lambda e: e.activation(
                        out=ext[:, j, 400:528], in_=ps[:, 0:128], func=AF.Copy), deps=[ev] + list(war) + pre_evs)
                    dst = ext[:, j, 528:848].rearrange("p (q e) -> p q e", e=80)[:, :, 16:80]
                    src = ps[:, 128:384].rearrange("p (q e) -> p q e", e=64)
                    ev2 = B.op("act", lambda e: e.activation(out=dst, in_=src, func=AF.Copy), deps=[ev])
                    free_bank(bi, ev2)
                    xb_evs += [ev1, ev2]
                else:
                    ev1 = B.op("act", lambda e: e.activation(
                        out=ext[:, j, c0:c0 + n], in_=ps[:, 0:n], func=AF.Copy), deps=[ev] + list(war))
                    free_bank(bi, ev1)
                    xb_evs.append(ev1)
            ring_release(ib, ev)
        save_evs = []
        if not is_c:
            save_evs.append(B.op("dve", lambda e: e.tensor_copy(out=xb_save[:], in_=ext[:, :, 768:784]),
                                 deps=xb_evs + pre_evs))
        else:
            e1 = B.op("dve", lambda e: e.tensor_copy(out=sp_prompt_sb[:], in_=ext[:, :, 512:528]),
                      deps=xb_evs + pre_evs)
            src = ext[:, :, 528:848].rearrange("p j (q e) -> p j q e", e=80)[:, :, :, 64:80]
            e2 = B.op("dve", lambda e: e.tensor_copy(out=sp_sample_sb[:], in_=src), deps=xb_evs + pre_evs)
            B.dma("sp", sp_prompt_o, sp_prompt_sb[:], s_out, deps=[e1])
            B.dma("sp", sp_sample_o, sp_sample_sb[:], s_out, deps=[e2])
            save_evs += [e1, e2]
        W = EW if is_c else TP
        d_evs = []
        chain_in = xb_evs + pre_evs + list(war)
        for j in range(8):
            g = j // 2
            w = 2 << g
            e = B.op(_CHAIN_ENG, lambda e_: e_.tensor_tensor(
                out=tmpA[:, 1:W], in0=ext[:, j, 1:W], in1=ext[:, j, 0:W - 1], op=ALU.add),
                deps=chain_in + d_evs[-1:])
            S = tmpA
            if w >= 4:
                e = B.op(_CHAIN_ENG, lambda e_: e_.tensor_tensor(
                    out=tmpB[:, 3:W], in0=tmpA[:, 3:W], in1=tmpA[:, 1:W - 2], op=ALU.add), deps=[e])
                S = tmpB
            if w >= 8:
                e = B.op(_CHAIN_ENG, lambda e_: e_.tensor_tensor(
                    out=tmpA[:, 7:W], in0=tmpB[:, 7:W], in1=tmpB[:, 3:W - 4], op=ALU.add), deps=[e])
                S = tmpA
            if w >= 16:
                e = B.op(_CHAIN_ENG, lambda e_: e_.tensor_tensor(
                    out=tmpB[:, 15:W], in0=tmpA[:, 15:W], in1=tmpA[:, 7:W - 8], op=ALU.add), deps=[e])
                S = tmpB
            npr = 528 if is_c else TP
            ed = B.op("dve", lambda e_, S=S: e_.scalar_tensor_tensor(
                out=d_buf[:, j, 16:npr], in0=S[:, 16:npr], scalar=1.0 / w, in1=ext[:, j, 16:npr],
                op0=ALU.mult, op1=ALU.subtract), deps=[e])
            if is_c:
                Sv = S[:, 528:848].rearrange("p (q e) -> p q e", e=80)[:, :, 16:80]
                Ev = ext[:, j, 528:848].rearrange("p (q e) -> p q e", e=80)[:, :, 16:80]
                Dv = d_buf[:, j, 528:784].rearrange("p (q e) -> p q e", e=64)
                ed = B.op("dve", lambda e_, Sv=Sv, Ev=Ev, Dv=Dv: e_.scalar_tensor_tensor(
                    out=Dv, in0=Sv, scalar=1.0 / w, in1=Ev, op0=ALU.mult, op1=ALU.subtract), deps=[e])
            if is_a:
                e3 = B.op("dve", lambda e_, S=S: e_.tensor_tensor(
                    out=tmp16[:], in0=S[:, 16:32], in1=invcnt[:, g, :], op=ALU.mult), deps=[e, ev_cst])
                ed = B.op("dve", lambda e_: e_.tensor_tensor(
                    out=d_buf[:, j, 16:32], in0=tmp16[:], in1=ext[:, j, 16:32], op=ALU.subtract), deps=[e3, ed])
            d_evs.append(ed)
        u_evs = []
        for j in range(8):
            ib, slot, rdy = ring_acquire()
            for si, (c0, n) in enumerate(SUB):
                bi, ps, bf_ = get_bank()
                for k in range(KC):
                    dk = ([rdy] + (list(bf_) if bf_ else [])) if k == 0 else []
                    if j == 0:
                        dk.append(hm_ev[(si, k)])
                    ev = B.op("pe", lambda e, k=k: e.matmul(
                        ps[:, 0:n], lhsT=slot[:, k, :], rhs=h[:, k, c0:c0 + n],
                        start=(k == 0), stop=(k == KC - 1)),
                        deps=dk, sig=(k == KC - 1))
                ev_u = B.op("act", lambda e: e.activation(
                    out=u_buf[:, j, c0:c0 + n], in_=ps[:, 0:n], func=AF.Gelu), deps=[ev] + list(war))
                free_bank(bi, ev_u)
                u_evs.append(ev_u)
            ring_release(ib, ev)
        a_evs = []
        b_evs = []
        v_free = [None, None, None]
        vnb_free = [None, None, None]
        vnf_free = [None, None]
        bout_war = d_evs + save_evs
        pendq = []
        vstate = {"tcount": 0, "last_pe": None}

        def emit_v(hg, ti, blks):
            tc0 = 16 + 128 * ti
            sample = is_c and ti >= 4
            bsel = vstate["tcount"] % 3
            vstate["tcount"] += 1
            fsel = ti % 2
            bi, ps, bf_ = get_bank()
            first = True
            for c, (ib, slot, rdy) in enumerate(blks):
                for k in range(KC):
                    deps = []
                    if k == 0 and ti == 0:
                        deps.append(rdy)
                    if first:
                        deps += hm_evs + (list(bf_) if bf_ else [])
                        first = False
                    ev = B.op("pe", lambda e: e.matmul(
                        ps[:, c * 128:(c + 1) * 128], lhsT=h[:, k, tc0:tc0 + 128], rhs=slot[:, k, :],
                        start=(k == 0), stop=(k == KC - 1)),
                        deps=deps, sig=(c == 3 and k == KC - 1))
            vstate["last_pe"] = ev
            ev_v = B.op("act", lambda e: e.activation(
                out=v_sb[bsel][:], in_=ps[:, :], func=AF.Gelu), deps=[ev, v_free[bsel]])
            free_bank(bi, ev_v)
            for hh in range(4):
                e_sq = B.op("act", lambda e: e.activation(
                    out=scr[:, hh * 128:(hh + 1) * 128], in_=v_sb[bsel][:, hh * 128:(hh + 1) * 128],
                    func=AF.Square, accum_out=ss4[:, hh:hh + 1]), deps=[ev_v])
            e_rt = B.op("act", lambda e: e.activation(
                out=r4[:], in_=ss4[:], func=AF.Sqrt, bias=epsb[:, 0:1], scale=1.0 / 128),
                deps=[e_sq, ev_eps, state.get("r4_free")])
            e_rc = B.op("dve", lambda e: e.reciprocal(out=r4[:], in_=r4[:]), deps=[e_rt])
            nb0 = hg * 512
            if sample:
                for hh in range(4):
                    e7 = B.op("dve", lambda e: e.scalar_tensor_tensor(
                        out=vn_f32[fsel][:, hh * 128:(hh + 1) * 128],
                        in0=v_sb[bsel][:, hh * 128:(hh + 1) * 128], scalar=r4[:, hh:hh + 1],
                        in1=norm_bc[:, nb0 + hh * 128:nb0 + (hh + 1) * 128], op0=ALU.mult, op1=ALU.mult),
                        deps=[e_rc, vnf_free[fsel], ev_setup])
                row0 = (ti - 4) * 128
                vnf_free[fsel] = B.dma("sp", sv_sample_o[row0:row0 + 128, hg * 512:(hg + 1) * 512],
                                       vn_f32[fsel][:], s_sv[fsel], deps=[e7])
            for hh in range(4):
                e8 = B.op("dve", lambda e: e.scalar_tensor_tensor(
                    out=vn_bf[bsel][:, hh * 128:(hh + 1) * 128],
                    in0=v_sb[bsel][:, hh * 128:(hh + 1) * 128], scalar=r4[:, hh:hh + 1],
                    in1=norm_bc[:, nb0 + hh * 128:nb0 + (hh + 1) * 128], op0=ALU.mult, op1=ALU.mult),
                    deps=[e_rc, vnb_free[bsel], ev_setup])
            v_free[bsel] = e8
            state["r4_free"] = e8
            pendq.append((hg, bsel, e8, sample, tc0))

        def emit_sgu():
            hg, pb, pev, psample, ptc0 = pendq.pop(0)
            var = 1 if psample else 0
            bi2, ps2, bf2 = get_bank()
            for hh in range(4):
                hd = hg * 4 + hh
                deps = [pev, ev_wm, ev_blo, ev_setup_g]
                if hh == 0:
                    deps += (list(bf2) if bf2 else [])
                o = ps2[:, hh * 128:(hh + 1) * 128]
                B.op("pe", lambda e: e.matmul(
                    o, lhsT=vn_bf[pb][:, hh * 128:(hh + 1) * 128], rhs=wmT[:, var, hd, :],
                    start=True, stop=False), deps=deps, sig=False)
                B.op("pe", lambda e: e.matmul(
                    o, lhsT=sel[:, hd, :], rhs=b_sb[:, var, 0, :], start=False, stop=False), sig=False)
                ev_s = B.op("pe", lambda e: e.matmul(
                    o, lhsT=sel[:, hd, :], rhs=b_sb[:, var, 1, :], start=False, stop=True),
                    sig=(hh == 3))
            vnb_free[pb] = ev_s
            uv = u_buf[:, hg * 4:(hg + 1) * 4, ptc0:ptc0 + 128]
            ev_a = B.op("dve", lambda e: e.tensor_tensor(
                out=uv, in0=ps2[:, :].rearrange("p (h t) -> p h t", t=128), in1=uv, op=ALU.mult),
                deps=[ev_s] + u_evs)
            free_bank(bi2, ev_a)
            a_evs.append(ev_a)

        def emit_poolw(groups):
            for (g, dd, c0, n) in groups:
                bi, ps, bf_ = get_bank()
                for cc in range(2):
                    ev = B.op("pe", lambda e: e.matmul(
                        ps[:, 0:n], lhsT=poolw[:, g, cc, dd * 128:(dd + 1) * 128],
                        rhs=d_buf[:, 2 * g + cc, c0:c0 + n], start=(cc == 0), stop=(cc == 1)),
                        deps=(d_evs + [ev_setup_g] + (list(bf_) if bf_ else [])) if cc == 0 else (),
                        sig=(cc == 1))
                jj = 2 * g + dd
                ev_b = B.op("act", lambda e: e.activation(
                    out=b_out[:, jj, c0:c0 + n], in_=ps[:, 0:n], func=AF.Copy,
                    scale=pool_scale[:, jj:jj + 1]), deps=[ev, ev_cst] + bout_war)
                free_bank(bi, ev_b)
                b_evs.append(ev_b)

        pw_groups = [(g, dd, c0, n) for g in range(4) for dd in range(2) for (c0, n) in SUB]
        if _V_CROSS:
            vsteps = [(hg, ti) for hg in range(2) for ti in range(6)]
            blks = None
            for idx in range(len(vsteps) + 2):
                if idx < len(vsteps):
                    hg, ti = vsteps[idx]
                    if ti == 0:
                        blks = [ring_acquire() for _ in range(4)]
                    emit_v(hg, ti, blks)
                    if ti == 5:
                        for (ib, slot, rdy) in blks:
                            ring_release(ib, vstate["last_pe"])
                else:
                    half = len(pw_groups) // 2
                    emit_poolw(pw_groups[:half] if idx == len(vsteps) else pw_groups[half:])
                if idx >= 2:
                    emit_sgu()
        else:
            for hg in range(2):
                blks = [ring_acquire() for _ in range(4)]
                for ti in range(8):
                    if ti < 6:
                        emit_v(hg, ti, blks)
                        if ti == 5:
                            for (ib, slot, rdy) in blks:
                                ring_release(ib, vstate["last_pe"])
                    if ti >= 2:
                        emit_sgu()
            emit_poolw(pw_groups)
        assert not pendq
        x_out = []
        acc = StatsAcc(SUB)
        for dch in range(KC):
            ib, slot, rdy = ring_acquire()
            for si, (c0, n) in enumerate(SUB):
                bi, ps, bf_ = get_bank()
                for m in range(16):
                    rhs = u_buf[:, m, c0:c0 + n] if m < 8 else b_out[:, m - 8, c0:c0 + n]
                    deps = []
                    if m == 0:
                        deps = [rdy] + a_evs + (list(bf_) if bf_ else [])
                    if m == 8:
                        deps = b_evs
                    ev = B.op("pe", lambda e, m=m, rhs=rhs: e.matmul(
                        ps[:, 0:n], lhsT=slot[:, m, :], rhs=rhs, start=(m == 0), stop=(m == 15)),
                        deps=deps, sig=(m == 15))
                acc.flush()
                ev_x = B.op("dve", lambda e: e.tensor_tensor(
                    out=x[:, dch, c0:c0 + n], in0=ps[:, 0:n], in1=x[:, dch, c0:c0 + n], op=ALU.add),
                    deps=[ev, x_evs])
                free_bank(bi, ev_x)
                x_out.append(ev_x)
                acc.add(si, dch, ev_x)
            ring_release(ib, ev)
        acc.flush(keep_last=True)
        state["inter_war"] = [ev]
        return x_out, acc

    y_store_prev = None
    for p in range(npass):
        if p > 0:
            x_loaded = x_loaded_next

        def x_evs(k, x_loaded=x_loaded):
            return list(x_loaded) if k == "all" else [x_loaded[k // 4]]

        if y_store_prev is not None:
            state["inter_war"] = list(state["inter_war"]) + [y_store_prev]
        subs1 = SUBH if p == 0 else SUB
        x1, acc1 = ffn(0, 0, subs1, x_evs)
        x2, acc2 = mixer(p, subs1, x1, acc1)
        x3, acc3 = ffn(1, 2, SUB, x2, acc2,
                       after_gateup=(lambda ev, p=p: prefetch_x(p + 1, ev)) if p + 1 < npass else None)
        y_ev = rmsnorm_fm(3, SUB, x3, lambda k, c0, n: ystage[:, k, c0 - 16:c0 - 16 + n],
                          extra_deps=state["inter_war"], k_major=True, acc=acc3)
        for q in range(4):
            x_war[q] = [y_ev[(si, k)] for si in range(2) for k in range(4 * q, 4 * q + 4)]
        for q in (1, 0):
            y_store_prev = B.dma("sp", y_fm[p, :, 8 * q:8 * q + 8, :], ystage[:, 8 * q:8 * q + 8, :], s_y[p],
                                 deps=[y_ev[(si, k)] for si in range(2) for k in range(8 * q, 8 * q + 8)])
            if q == 1 and p + 1 < npass:
                x_loaded_next = load_x(p + 1)
    B.wait("sp", [(s_y[p], s_y[p].total) for p in range(npass)] +
           [(s_out, s_out.total), (s_sv[0], s_sv[0].total), (s_sv[1], s_sv[1].total)])
    assert ring_state["acquired"] == len(wq), (ring_state, len(wq))
    return nc


_PROGRAM = None


def _fm(tok):
    n, f = tok.shape
    return np.ascontiguousarray(tok.reshape(n, f // 128, 128).transpose(2, 1, 0))


def _tm(fm):
    q, k, n = fm.shape
    return fm.transpose(2, 1, 0).reshape(n, k * 128)


def kernel(x_prompt, x_sample, cache_pool, ffn1_norm, ffn1_w_gate, ffn1_w_up, ffn1_w_down, mix_norm,
           w_in, sgu_norm, sgu_w, sgu_b, pool_w, pool_scale, w_out, ffn2_norm, ffn2_w_gate, ffn2_w_up,
           ffn2_w_down, final_norm):
    global _PROGRAM
    if _PROGRAM is None:
        _PROGRAM = build_program()
    nc = _PROGRAM
    in_maps = _prep(x_prompt, x_sample, cache_pool, ffn1_norm, ffn1_w_gate, ffn1_w_up, ffn1_w_down, mix_norm,
                    w_in, sgu_norm, sgu_w, sgu_b, pool_w, pool_scale, w_out, ffn2_norm, ffn2_w_gate, ffn2_w_up,
                    ffn2_w_down, final_norm)
    res = run_bass_kernel_spmd(nc, in_maps, core_ids=list(range(NCORES)))
    return _post(res.results)


def _prep(x_prompt, x_sample, cache_pool, ffn1_norm, ffn1_w_gate, ffn1_w_up, ffn1_w_down, mix_norm,
          w_in, sgu_norm, sgu_w, sgu_b, pool_w, pool_scale, w_out, ffn2_norm, ffn2_w_gate, ffn2_w_up,
          ffn2_w_down, final_norm):
    f32 = np.float32
    a = lambda t: np.asarray(t, dtype=f32)
    x_prompt, x_sample, cache_pool = a(x_prompt), a(x_sample), a(cache_pool)

    shared = {
        "ffn1_w_gate": np.ascontiguousarray(a(ffn1_w_gate)[0]),
        "ffn1_w_up": np.ascontiguousarray(a(ffn1_w_up)[0]),
        "ffn1_w_down": np.ascontiguousarray(a(ffn1_w_down)[0]),
        "ffn2_w_gate": np.ascontiguousarray(a(ffn2_w_gate)[0]),
        "ffn2_w_up": np.ascontiguousarray(a(ffn2_w_up)[0]),
        "ffn2_w_down": np.ascontiguousarray(a(ffn2_w_down)[0]),
        "w_in": np.ascontiguousarray(a(w_in)[0]),
        "w_out": np.ascontiguousarray(a(w_out)[0]),
        "sgu_norm": np.ascontiguousarray(a(sgu_norm)[0].reshape(1, 1024)),
        "pool_w": np.ascontiguousarray(a(pool_w)[0]),
    }
    sw = a(sgu_w)[0]
    wT2 = np.zeros((128, 2, 8, 128), f32)
    wT2[:, 0] = sw.transpose(2, 0, 1)
    blk = sw[:, :64, :64].transpose(2, 0, 1)
    wT2[0:64, 1, :, 0:64] = blk
    wT2[64:128, 1, :, 64:128] = blk
    s_i = np.arange(128)[:, None]
    t_i = np.arange(128)[None, :]
    mask2 = np.zeros((128, 2, 128), f32)
    mask2[:, 0] = (s_i <= t_i)
    mask2[:, 1] = ((s_i // 64) == (t_i // 64)) & ((s_i % 64) <= (t_i % 64))
    sb_ = a(sgu_b)[0]
    b2 = np.zeros((128, 2, 128), f32)
    b2[0:8, 0] = sb_
    b2[0:8, 1, 0:64] = sb_[:, :64]
    b2[0:8, 1, 64:128] = sb_[:, :64]
    sel = np.zeros((128, 8, 128), f32)
    for hh in range(8):
        sel[hh, hh, :] = 1.0
    shared.update({"sgu_wT2": wT2, "sgu_mask2": mask2, "sgu_b2": b2, "sel": sel})
    gains = np.concatenate([a(g).reshape(16, 128).T for g in
                            (a(ffn1_norm)[0], a(mix_norm)[0], a(ffn2_norm)[0], a(final_norm))], axis=1)
    pscale = a(pool_scale)[0].reshape(8, 128).T

    in_maps = []
    for c in range(NCORES):
        b, hf = c // 2, c % 2
        base = hf * 2048
        xf = np.zeros((NPASS, 128, KC, TP), f32)
        tokA = np.zeros((TP, D), f32)
        if hf == 1:
            tokA[0:16] = x_prompt[b, base - 16:base]
        tokA[16:] = x_prompt[b, base:base + 768]
        xf[0] = _fm(tokA)
        xf[1, :, :, 16:] = _fm(x_prompt[b, base + 768:base + 1536])
        tokC = np.concatenate([x_prompt[b, base + 1536:base + 2048],
                               x_sample[4 * c:4 * c + 4].reshape(256, D)], axis=0)
        xf[2, :, :, 16:] = _fm(tokC)
        pref = np.zeros((128, 8, 4, 16), f32)
        for q in range(4):
            rows = np.zeros((16, 1024), f32)
            rows[1:] = cache_pool[0, 4 * c + q]
            pref[:, :, q, :] = _fm(rows)
        inv = np.zeros((128, 4, 16), f32)
        for g, w in enumerate((2, 4, 8, 16)):
            if hf == 0:
                inv[:, g, :] = 1.0 / np.minimum(w, np.arange(16) + 1)
            else:
                inv[:, g, :] = 1.0 / w
        cst = np.concatenate([gains, pscale, inv.reshape(128, 64)], axis=1).astype(f32)
        m = dict(shared)
        m.update({"x_fm": xf, "pool_pref": pref, "cst": np.ascontiguousarray(cst)})
        in_maps.append(m)
    return in_maps


def _post(results):
    f32 = np.float32
    y_prompt = np.zeros((4, 4096, D), f32)
    y_sample = np.zeros((32, 64, D), f32)
    sp_p = np.zeros((1, 4, 15, 1024), f32)
    sp_s = np.zeros((1, 32, 15, 1024), f32)
    sv_s = np.zeros((1, 32, 64, 1024), f32)
    for c in range(NCORES):
        r = results[c]
        if r is None:
            continue
        b, hf = c // 2, c % 2
        base = hf * 2048
        yf = r["y_fm"]
        y_prompt[b, base:base + 768] = _tm(yf[0])
        y_prompt[b, base + 768:base + 1536] = _tm(yf[1])
        tc = _tm(yf[2])
        y_prompt[b, base + 1536:base + 2048] = tc[:512]
        y_sample[4 * c:4 * c + 4] = tc[512:].reshape(4, 64, D)
        if hf == 1:
            sp_p[0, b] = _tm(r["sp_prompt"])[1:]
        for q in range(4):
            sp_s[0, 4 * c + q] = _tm(r["sp_sample"][:, :, q, :])[1:]
        sv_s[0, 4 * c:4 * c + 4] = r["sv_sample"].reshape(4, 64, 1024)
    return (y_prompt, y_sample, sp_p, sp_s, sv_s)
```

```python
import numpy as np
import concourse.bass as bass
import concourse.mybir as mybir
from concourse.bass_utils import run_bass_kernel_spmd

F32 = mybir.dt.float32
BF16 = mybir.dt.bfloat16
AF = mybir.ActivationFunctionType
ALU = mybir.AluOpType
AX = mybir.AxisListType

D = 2048
DFF = 5632
KC = 16
FC = 44
T = 768
HALO = 16
TP = T + HALO
EW = 848
NPASS = 3
NS = 6
EPS = 1e-6
NCORES = 8
_ACCUM = True
_CHAIN_ENG = "dve"
_V_CROSS = False
SUB = [(16, 384), (400, 384)]
SUBH = [(0, 400), (400, 384)]


class _Sem:
    _n = 0

    def __init__(self, nc, name):
        self.h = nc.alloc_semaphore(name)
        self.id = _Sem._n
        _Sem._n += 1
        self.total = 0


class _Eng:
    def __init__(self, nc, name, eng):
        self.eng = eng
        self.sem = _Sem(nc, "e_" + name)
        self.count = 0
        self.waited = {}


class Builder:
    def __init__(self):
        self.nc = bass.Bass("TRN2", target_bir_lowering=False)
        nc = self.nc
        self.E = {
            "pe": _Eng(nc, "pe", nc.tensor),
            "act": _Eng(nc, "act", nc.scalar),
            "dve": _Eng(nc, "dve", nc.vector),
            "pool": _Eng(nc, "pool", nc.gpsimd),
            "sp": _Eng(nc, "sp", nc.sync),
        }

    def wait(self, en, deps):
        E = self.E[en]
        best = {}
        for ev in deps:
            if ev is None:
                continue
            if isinstance(ev, list):
                for e2 in ev:
                    if e2 is not None:
                        s, v = e2
                        if best.get(s.id, (None, 0))[1] < v:
                            best[s.id] = (s, v)
                continue
            s, v = ev
            if best.get(s.id, (None, 0))[1] < v:
                best[s.id] = (s, v)
        for sid, (s, v) in best.items():
            if E.waited.get(sid, 0) < v:
                E.eng.wait_ge(s.h, v)
                E.waited[sid] = v

    def op(self, en, fn, deps=(), sig=True):
        self.wait(en, deps)
        E = self.E[en]
        ins = fn(E.eng)
        if sig:
            E.count += 1
            ins.then_inc(E.sem.h, 1)
            return (E.sem, E.count)
        return None

    def last_event(self, en):
        E = self.E[en]
        return (E.sem, E.count) if E.count > 0 else None

    def dma(self, en, out, in_, sem, deps=()):
        self.wait(en, deps)
        E = self.E[en]
        E.eng.dma_start(out=out, in_=in_).then_inc(sem.h, 16)
        sem.total += 16
        return (sem, sem.total)


def build_program(npass=NPASS):
    B = Builder()
    nc = B.nc

    def din(name, shape):
        return nc.dram_tensor(name, list(shape), F32, kind="ExternalInput").ap()

    def dout(name, shape):
        return nc.dram_tensor(name, list(shape), F32, kind="ExternalOutput").ap()

    x_fm = din("x_fm", [NPASS, 128, KC, TP])
    pool_pref = din("pool_pref", [128, 8, 4, 16])
    cst_d = din("cst", [128, 136])
    wT2_d = din("sgu_wT2", [128, 2, 8, 128])
    mask2_d = din("sgu_mask2", [128, 2, 128])
    b2_d = din("sgu_b2", [128, 2, 128])
    sel_d = din("sel", [128, 8, 128])
    snorm_d = din("sgu_norm", [1, 1024])
    poolw_d = din("pool_w", [4, 256, 256])
    Wg = [din("ffn1_w_gate", [D, DFF]), din("ffn2_w_gate", [D, DFF])]
    Wu = [din("ffn1_w_up", [D, DFF]), din("ffn2_w_up", [D, DFF])]
    Wd = [din("ffn1_w_down", [DFF, D]), din("ffn2_w_down", [DFF, D])]
    w_in = din("w_in", [D, 3072])
    w_out = din("w_out", [D, D])

    y_fm = dout("y_fm", [NPASS, 128, KC, T])
    sp_prompt_o = dout("sp_prompt", [128, 8, 16])
    sp_sample_o = dout("sp_sample", [128, 8, 4, 16])
    sv_sample_o = dout("sv_sample", [256, 1024])

    def sb(name, shape, dt):
        return nc.alloc_sbuf_tensor(name, list(shape), dt)

    x = sb("x", [128, KC, TP], F32)
    h = sb("h", [128, KC, TP], BF16)
    inter_t = sb("inter", [128, FC * TP], BF16)
    ring_slots = [sb(f"ring{i}", [128, KC, 128], BF16) for i in range(NS)]
    v_sb = [sb(f"v_sb{i}", [128, 512], F32) for i in range(3)]
    scr = sb("scr", [128, 512], BF16)
    vn_bf = [sb(f"vn_bf{i}", [128, 512], BF16) for i in range(3)]
    vn_f32 = [sb(f"vn_f32{i}", [128, 512], F32) for i in range(2)]
    rstd = sb("rstd", [128, TP], F32)
    sqtmp = [sb(f"sqtmp{i}", [128, 400], F32) for i in range(2)]
    silu_tmp = [sb(f"silu{i}", [128, 400], F32) for i in range(2)]
    cst = sb("cst_sb", [128, 136], F32)
    ones_f32 = sb("ones_f32", [128, 128], F32)
    wmT = sb("wmT", [128, 2, 8, 128], BF16)
    sel = sb("sel_sb", [128, 8, 128], BF16)
    b_sb = sb("b_sb", [128, 2, 2, 128], BF16)
    norm_bc = sb("norm_bc", [128, 1024], F32)
    poolw = sb("poolw", [128, 4, 2, 256], BF16)
    epsb = sb("epsb", [128, 1], F32)
    ss4 = sb("ss4", [128, 4], F32)
    r4 = sb("r4", [128, 4], F32)
    tmp16 = sb("tmp16", [128, 16], F32)
    xb_save = sb("xb_save", [128, 8, 16], F32)
    sp_prompt_sb = sb("sp_prompt_sb", [128, 8, 16], F32)
    sp_sample_sb = sb("sp_sample_sb", [128, 8, 4, 16], F32)

    inter = inter_t[:, :].rearrange("p (f t) -> p f t", t=TP)
    u_buf = inter_t[:, 0:8 * TP].rearrange("p (j t) -> p j t", t=TP)
    d_buf = inter_t[:, 8 * TP:16 * TP].rearrange("p (j t) -> p j t", t=TP)
    EXT0 = 16 * TP
    ext = inter_t[:, EXT0:EXT0 + 2 * 8 * EW].bitcast(F32).rearrange("p (j t) -> p j t", t=EW)
    TA0 = EXT0 + 2 * 8 * EW
    tmpA = inter_t[:, TA0:TA0 + 2 * EW].bitcast(F32)
    tmpB = inter_t[:, TA0 + 2 * EW:TA0 + 4 * EW].bitcast(F32)
    b_out = inter_t[:, EXT0:EXT0 + 8 * TP].rearrange("p (j t) -> p j t", t=TP)
    ystage = inter_t[:, 0:2 * KC * T].bitcast(F32).rearrange("p (k t) -> p k t", t=T)
    su_wT2 = inter_t[:, 0:4096].bitcast(F32).rearrange("p (v h t) -> p v h t", v=2, h=8)
    su_mask = inter_t[:, 4096:4096 + 512].bitcast(F32).rearrange("p (v t) -> p v t", v=2)
    su_b2 = inter_t[:, 4608:4608 + 512].bitcast(F32).rearrange("p (v t) -> p v t", v=2)
    su_bhi = inter_t[:, 5120:5120 + 512].bitcast(F32).rearrange("p (v t) -> p v t", v=2)

    hstage = h[:, :, :].rearrange("p k t -> p (k t)")[:, 0:2 * 8 * T].bitcast(F32).rearrange("p (k t) -> p k t", t=T)
    gains = cst[:, 0:64]
    pool_scale = cst[:, 64:72]
    invcnt = cst[:, 72:136].rearrange("p (g t) -> p g t", t=16)

    banks = [nc.alloc_psum_tensor(f"bank{i}", [128, 512], F32) for i in range(8)]
    bank_free = [None] * 8
    bank_ctr = [0]

    bank_reserved = set()

    def get_bank(reserve=False):
        while (bank_ctr[0] % 8) in bank_reserved:
            bank_ctr[0] += 1
        i = bank_ctr[0] % 8
        bank_ctr[0] += 1
        if reserve:
            bank_reserved.add(i)
        return i, banks[i], bank_free[i]

    def free_bank(i, evs):
        bank_free[i] = evs if isinstance(evs, list) else [evs]

    s_setup = _Sem(nc, "d_setup")
    s_setup_g = _Sem(nc, "d_setup_g")
    s_x = [[_Sem(nc, f"d_x{p}_{q}") for q in range(4)] for p in range(NPASS)]
    s_pref = _Sem(nc, "d_pref")
    s_xpre = [[_Sem(nc, f"d_xpre{p}_{q}") for q in range(2)] for p in range(NPASS)]
    s_out = _Sem(nc, "d_out")
    s_y = [_Sem(nc, f"d_y{p}") for p in range(NPASS)]
    s_sv = [_Sem(nc, "d_sv0"), _Sem(nc, "d_sv1")]
    ring_sem = [_Sem(nc, f"d_ring{i}") for i in range(NS)]

    wq = []
    ring_state = {"issued": 0, "acquired": 0, "released": 0}
    ring_last_use = [None] * NS

    def ring_issue():
        while ring_state["issued"] < min(len(wq), ring_state["released"] + NS):
            i = ring_state["issued"]
            s = i % NS
            src, nk = wq[i]
            B.dma("pool", ring_slots[s][:, 0:nk, :], src, ring_sem[s], deps=[ring_last_use[s]])
            ring_state["issued"] += 1

    def ring_acquire():
        i = ring_state["acquired"]
        ring_state["acquired"] += 1
        assert i < ring_state["issued"], "ring block not issued yet"
        s = i % NS
        return i, ring_slots[s], (ring_sem[s], 16 * (i // NS + 1))

    def ring_release(i, ev):
        assert i == ring_state["released"]
        ring_last_use[i % NS] = ev
        ring_state["released"] += 1
        ring_issue()

    def colblk(W, c0, r0=0, nk=KC):
        return (W[r0:r0 + nk * 128, c0:c0 + 128].rearrange("(k p) c -> p k c", p=128), nk)

    for p in range(npass):
        for wi in range(2):
            if wi == 1:
                for j in range(8):
                    wq.append(colblk(w_in, 2048 + j * 128))
                for j in range(8):
                    wq.append(colblk(w_in, j * 128))
                for j in range(8):
                    wq.append(colblk(w_in, 1024 + j * 128))
                for j in range(16):
                    wq.append(colblk(w_out, j * 128))
            for f in range(FC):
                wq.append(colblk(Wg[wi], f * 128))
                wq.append(colblk(Wu[wi], f * 128))
            for dch in range(KC):
                wq.append(colblk(Wd[wi], dch * 128, 0, 16))
                wq.append(colblk(Wd[wi], dch * 128, 2048, 16))
                wq.append(colblk(Wd[wi], dch * 128, 4096, 12))

    x_war = {q: None for q in range(4)}

    xpre = {}

    def prefetch_x(p, h_free_ev):
        xpre[p] = [B.dma("sp", hstage[:, 4 * q:4 * q + 4, :], x_fm[p, :, 4 * q:4 * q + 4, 16:TP],
                         s_xpre[p][q], deps=[h_free_ev]) for q in range(2)]

    def load_x(p):
        c_lo = 0 if p == 0 else 16
        evs = []
        for q in range(4):
            if p in xpre and q < 2:
                evs.append(B.op("act", lambda e: e.activation(
                    out=x[:, 4 * q:4 * q + 4, 16:TP], in_=hstage[:, 4 * q:4 * q + 4, :], func=AF.Copy),
                    deps=[x_war[q], xpre[p][q]]))
            else:
                evs.append(B.dma("sp", x[:, 4 * q:4 * q + 4, c_lo:TP], x_fm[p, :, 4 * q:4 * q + 4, c_lo:TP],
                                 s_x[p][q], deps=[x_war[q]]))
        if p in xpre:
            state["h_war"] = evs[0:2]
            state["sq_stage"] = list(xpre[p])
        return evs

    x_loaded = load_x(0)
    ev_cst = B.dma("sp", cst[:], cst_d, s_setup)
    B.dma("sp", su_wT2, wT2_d, s_setup)
    B.dma("sp", su_mask, mask2_d, s_setup)
    B.dma("sp", su_b2, b2_d, s_setup)
    ev_setup = B.dma("sp", norm_bc[:], bass.AP(snorm_d.tensor, 0, [[0, 128], [1, 1024]]), s_setup)
    ev_cst = ev_setup
    B.dma("pool", sel[:], sel_d, s_setup_g)
    ev_setup_g = B.dma("pool", poolw[:], poolw_d.rearrange("g (cc p) d -> p g cc d", p=128), s_setup_g)
    ring_issue()

    ev_ones = B.op("dve", lambda e: e.memset(ones_f32[:], 1.0 / D))
    ev_eps = B.op("dve", lambda e: e.memset(epsb[:], EPS))
    ev_wm = None
    for v in range(2):
        ev_wm = B.op("dve", lambda e, v=v: e.tensor_tensor(
            out=wmT[:, v, :, :], in0=su_wT2[:, v, :, :],
            in1=su_mask[:, v, :].unsqueeze(1).to_broadcast([128, 8, 128]), op=ALU.mult),
            deps=[ev_setup])
    ev_bhi = B.op("dve", lambda e: e.tensor_copy(out=b_sb[:, :, 0, :], in_=su_b2), deps=[ev_setup])
    ev_bhi2 = B.op("dve", lambda e: e.tensor_copy(out=su_bhi, in_=b_sb[:, :, 0, :]), deps=[ev_bhi])
    ev_blo = B.op("dve", lambda e: e.tensor_tensor(out=b_sb[:, :, 1, :], in0=su_b2, in1=su_bhi, op=ALU.subtract),
                  deps=[ev_bhi2])
    setup_done = [ev_ones, ev_eps, ev_wm, ev_blo, ev_setup, ev_setup_g, ev_cst]

    state = {"inter_war": [ev_blo, ev_wm]}

    sqst = {"free": [None, None], "cnt": 0}

    class StatsAcc:
        def __init__(self, subs):
            self.subs = subs
            self.n = [0] * len(subs)
            self.last = [None] * len(subs)
            self.pending = []

        def add(self, si, k, x_ev):
            c0, n = self.subs[si]
            if self.n[si] == 0:
                self.last[si] = B.op("act", lambda e: e.activation(
                    out=rstd[:, c0:c0 + n], in_=x[:, k, c0:c0 + n], func=AF.Square),
                    deps=[x_ev, state.get("rstd_readers")])
            else:
                b = sqst["cnt"] % 2
                sqst["cnt"] += 1
                assert all(pb != b for (_, pb, _) in self.pending)
                ev_sq = B.op("act", lambda e: e.activation(
                    out=sqtmp[b][:, 0:n], in_=x[:, k, c0:c0 + n], func=AF.Square),
                    deps=[x_ev, sqst["free"][b]])
                self.pending.append((si, b, ev_sq))
            self.n[si] += 1

        def flush(self, keep_last=False):
            while self.pending:
                if keep_last and len(self.pending) <= len(self.subs) and all(c == KC for c in self.n):
                    break
                si, b, ev_sq = self.pending.pop(0)
                c0, n = self.subs[si]
                ev = B.op("dve", lambda e: e.tensor_tensor(
                    out=rstd[:, c0:c0 + n], in0=rstd[:, c0:c0 + n], in1=sqtmp[b][:, 0:n], op=ALU.add),
                    deps=[ev_sq, self.last[si]])
                sqst["free"][b] = ev
                self.last[si] = ev

        def finalize(self, si):
            assert self.n[si] == KC
            c0, n = self.subs[si]
            tail = [pnd for pnd in self.pending if pnd[0] == si]
            self.pending = [pnd for pnd in self.pending if pnd[0] != si]
            bi, ps, bfree = get_bank()
            ev_mm = B.op("pe", lambda e: e.matmul(
                ps[:, 0:n], lhsT=ones_f32[:], rhs=rstd[:, c0:c0 + n], start=True, stop=(not tail)),
                deps=[self.last[si], ev_ones] + (list(bfree) if bfree else []))
            for i, (_, b, ev_sq) in enumerate(tail):
                ev_mm = B.op("pe", lambda e: e.matmul(
                    ps[:, 0:n], lhsT=ones_f32[:], rhs=sqtmp[b][:, 0:n], start=False, stop=(i == len(tail) - 1)),
                    deps=[ev_sq])
                sqst["free"][b] = ev_mm
            return bi, ps, ev_mm

    def xdep(x_evs, k):
        return x_evs(k) if callable(x_evs) else x_evs

    def rmsnorm_fm(gidx, subs, x_evs, out_fn, extra_deps=(), k_major=False, acc=None):
        out = {}
        pe_last = B.last_event("pe")
        rec = []
        assert acc is None or acc.subs == subs
        stage_sq = []
        for si, (c0, n) in enumerate(subs):
            if acc is not None:
                bi, ps, ev_mm = acc.finalize(si)
            else:
                bi, ps, bfree = get_bank()
                stage = state.get("sq_stage")
                for k in range(KC):
                    b = sqst["cnt"] % 2
                    sqst["cnt"] += 1
                    if stage is not None and k < 8 and si == 0:
                        src, sdep = hstage[:, k, c0 - 16:c0 - 16 + n], stage[k // 4]
                    else:
                        src, sdep = x[:, k, c0:c0 + n], xdep(x_evs, k)
                    ev_sq = B.op("act", lambda e: e.activation(
                        out=sqtmp[b][:, 0:n], in_=src, func=AF.Square),
                        deps=[sdep, sqst["free"][b]])
                    if stage is not None and k < 8 and si == 0:
                        stage_sq.append(ev_sq)
                    ev_mm = B.op("pe", lambda e: e.matmul(
                        ps[:, 0:n], lhsT=ones_f32[:], rhs=sqtmp[b][:, 0:n], start=(k == 0), stop=(k == KC - 1)),
                        deps=[ev_sq, ev_ones] + (list(bfree) if (k == 0 and bfree) else []))
                    sqst["free"][b] = ev_mm
                if si == len(subs) - 1:
                    state.pop("sq_stage", None)
            ev_s = B.op("act", lambda e: e.activation(
                out=rstd[:, c0:c0 + n], in_=ps[:, 0:n], func=AF.Sqrt, bias=epsb[:, 0:1], scale=1.0),
                deps=[ev_mm, ev_eps, state.get("rstd_readers")])
            free_bank(bi, ev_s)
            ev_r = B.op("dve", lambda e: e.reciprocal(out=rstd[:, c0:c0 + n], in_=rstd[:, c0:c0 + n]), deps=[ev_s])
            rec.append(ev_r)

            def emit_out(si, k, c0=c0, n=n, ev_r=ev_r):
                en = "dve"
                out[(si, k)] = B.op(en, lambda e: e.scalar_tensor_tensor(
                    out=out_fn(k, c0, n), in0=x[:, k, c0:c0 + n],
                    scalar=gains[:, gidx * 16 + k:gidx * 16 + k + 1], in1=rstd[:, c0:c0 + n],
                    op0=ALU.mult, op1=ALU.mult),
                    deps=[ev_r, xdep(x_evs, k), ev_cst, pe_last] + list(extra_deps) + stage_sq)
            if not k_major:
                for k in range(KC):
                    emit_out(si, k)
            else:
                rec[-1] = (emit_out, si)
        if k_major:
            for k in list(range(8, KC)) + list(range(8)):
                for (fn, si) in rec:
                    fn(si, k)
        state["rstd_readers"] = list(out.values())
        return out

    def ffn(wi, gidx, subs, x_evs, acc_in=None, after_gateup=None):
        h_ev = rmsnorm_fm(gidx, subs, x_evs, lambda k, c0, n: h[:, k, c0:c0 + n], acc=acc_in,
                          extra_deps=state.pop("h_war", []))
        x_all = x_evs("all") if callable(x_evs) else x_evs
        inter_evs = []
        silu_free = [None, None]
        step = 0
        first_write_deps = state["inter_war"]
        order = [(f, si) for f in range(FC) for si in range(len(subs))]
        if len(subs) == 2:
            order[0:4] = [(0, 0), (1, 0), (0, 1), (1, 1)]
        held = {}
        h_waited = set()
        last_pe = None
        for (f, si) in order:
            if f not in held:
                held[f] = (ring_acquire(), ring_acquire(), [0])
            (ig, slot_g, rdy_g), (iu, slot_u, rdy_u), nuse = held[f]
            first_use = (nuse[0] == 0)
            c0, n = subs[si]
            bg, psg, fg = get_bank()
            bu, psu, fu = get_bank()
            for k in range(KC):
                dk = (([rdy_g] if first_use else []) + (list(fg) if fg else [])) if k == 0 else []
                if si not in h_waited:
                    dk.append(h_ev[(si, k)])
                ev_g = B.op("pe", lambda e, k=k: e.matmul(
                    psg[:, 0:n], lhsT=slot_g[:, k, :], rhs=h[:, k, c0:c0 + n],
                    start=(k == 0), stop=(k == KC - 1)),
                    deps=dk, sig=(k == KC - 1))
            h_waited.add(si)
            for k in range(KC):
                ev_u = B.op("pe", lambda e, k=k: e.matmul(
                    psu[:, 0:n], lhsT=slot_u[:, k, :], rhs=h[:, k, c0:c0 + n],
                    start=(k == 0), stop=(k == KC - 1)),
                    deps=((([rdy_u] if first_use else []) + (list(fu) if fu else []))) if k == 0 else (),
                    sig=(k == KC - 1))
            last_pe = ev_u
            sbi = step % 2
            ev_a = B.op("act", lambda e, sbi=sbi: e.activation(
                out=silu_tmp[sbi][:, 0:n], in_=psg[:, 0:n], func=AF.Silu),
                deps=[ev_g, silu_free[sbi]])
            free_bank(bg, ev_a)
            ev_i = B.op("dve", lambda e, sbi=sbi: e.tensor_tensor(
                out=inter[:, f, c0:c0 + n], in0=silu_tmp[sbi][:, 0:n], in1=psu[:, 0:n], op=ALU.mult),
                deps=[ev_a, ev_u] + list(first_write_deps))
            first_write_deps = []
            free_bank(bu, ev_i)
            silu_free[sbi] = ev_i
            inter_evs.append(ev_i)
            step += 1
            nuse[0] += 1
            if nuse[0] == len(subs):
                ring_release(ig, last_pe)
                ring_release(iu, last_pe)
        inter_done = inter_evs[-len(subs):]
        if after_gateup is not None:
            after_gateup(last_pe)
        x_out = []
        acc = StatsAcc(subs)
        for dch in range(KC):
            blks = [ring_acquire() for _ in range(3)]
            last_pe = None
            for si, (c0, n) in enumerate(subs):
                bi, ps, bf_ = get_bank()
                cnt = 0
                for r, (ib, slot, rdy) in enumerate(blks):
                    nk = 16 if r < 2 else 12
                    for kk in range(nk):
                        f = r * 16 + kk
                        deps = []
                        if kk == 0:
                            deps.append(rdy)
                        if cnt == 0:
                            deps += inter_done + (list(bf_) if bf_ else [])
                        ev = B.op("pe", lambda e, slot=slot, kk=kk, f=f, cnt=cnt: e.matmul(
                            ps[:, 0:n], lhsT=slot[:, kk, :], rhs=inter[:, f, c0:c0 + n],
                            start=(cnt == 0), stop=(cnt == FC - 1)),
                            deps=deps, sig=(cnt == FC - 1))
                        cnt += 1
                last_pe = ev
                acc.flush()
                ev_x = B.op("dve", lambda e: e.scalar_tensor_tensor(
                    out=x[:, dch, c0:c0 + n], in0=ps[:, 0:n], scalar=0.5, in1=x[:, dch, c0:c0 + n],
                    op0=ALU.mult, op1=ALU.add), deps=[ev, x_all])
                free_bank(bi, ev_x)
                x_out.append(ev_x)
                acc.add(si, dch, ev_x)
            for (ib, slot, rdy) in blks:
                ring_release(ib, last_pe)
        acc.flush(keep_last=True)
        state["inter_war"] = [last_pe]
        return x_out, acc

    def mixer(p, subs_b, x_evs, acc_in=None):
        is_a = (p == 0)
        is_c = (p == 2)
        hm_ev = rmsnorm_fm(1, subs_b, x_evs, lambda k, c0, n: h[:, k, c0:c0 + n], acc=acc_in)
        hm_evs = list(hm_ev.values())
        war = state["inter_war"]
        pre_evs = []
        if p > 0:
            pre_evs.append(B.op("dve", lambda e: e.tensor_copy(out=ext[:, :, 0:16], in_=xb_save[:]),
                                deps=list(war) + hm_evs))
        if is_c:
            for q in range(4):
                ev_pf = B.dma("sp", ext[:, :, 528 + 80 * q:528 + 80 * q + 16], pool_pref[:, :, q, :], s_pref,
                              deps=list(war) + hm_evs)
            pre_evs.append(ev_pf)
        xb_evs = []
        xorder = [(j, si) for j in range(8) for si in range(len(subs_b))]
        if len(subs_b) == 2:
            xorder[0:4] = [(0, 0), (1, 0), (0, 1), (1, 1)]
        xheld = {}
        hm_waited = set()
        for (j, si) in xorder:
            if j not in xheld:
                xheld[j] = (ring_acquire(), [0])
            (ib, slot, rdy), nuse = xheld[j]
            c0, n = subs_b[si]
            bi, ps, bf_ = get_bank()
            for k in range(KC):
                dk = (([rdy] if nuse[0] == 0 else []) + (list(bf_) if bf_ else [])) if k == 0 else []
                if si not in hm_waited:
                    dk.append(hm_ev[(si, k)])
                ev = B.op("pe", lambda e, k=k: e.matmul(
                    ps[:, 0:n], lhsT=slot[:, k, :], rhs=h[:, k, c0:c0 + n],
                    start=(k == 0), stop=(k == KC - 1)),
                    deps=dk, sig=(k == KC - 1))
            hm_waited.add(si)
            if is_c and c0 == 400:
                ev1 = B.op("act", lambda e: e.activation(
                    out=ext[:, j, 400:528], in_=ps[:, 0:128], func=AF.Copy), deps=[ev] + list(war) + pre_evs)
                dst = ext[:, j, 528:848].rearrange("p (q e) -> p q e", e=80)[:, :, 16:80]
                src = ps[:, 128:384].rearrange("p (q e) -> p q e", e=64)
                ev2 = B.op("act", lambda e: e.activation(out=dst, in_=src, func=AF.Copy), deps=[ev])
                free_bank(bi, ev2)
                xb_evs += [ev1, ev2]
            else:
                ev1 = B.op("act", lambda e: e.activation(
                    out=ext[:, j, c0:c0 + n], in_=ps[:, 0:n], func=AF.Copy), deps=[ev] + list(war))
                free_bank(bi, ev1)
                xb_evs.append(ev1)
            nuse[0] += 1
            if nuse[0] == len(subs_b):
                ring_release(ib, ev)
        save_evs = []
        if not is_c:
            save_evs.append(B.op("dve", lambda e: e.tensor_copy(out=xb_save[:], in_=ext[:, :, 768:784]),
                                 deps=xb_evs + pre_evs))
        else:
            e1 = B.op("dve", lambda e: e.tensor_copy(out=sp_prompt_sb[:], in_=ext[:, :, 512:528]),
                      deps=xb_evs + pre_evs)
            src = ext[:, :, 528:848].rearrange("p j (q e) -> p j q e", e=80)[:, :, :, 64:80]
            e2 = B.op("dve", lambda e: e.tensor_copy(out=sp_sample_sb[:], in_=src), deps=xb_evs + pre_evs)
            B.dma("sp", sp_prompt_o, sp_prompt_sb[:], s_out, deps=[e1])
            B.dma("sp", sp_sample_o, sp_sample_sb[:], s_out, deps=[e2])
            save_evs += [e1, e2]
        W = EW if is_c else TP
        d_evs = []
        chain_in = xb_evs + pre_evs + list(war)
        for j in range(8):
            g = j // 2
            w = 2 << g
            e = B.op(_CHAIN_ENG, lambda e_: e_.tensor_tensor(
                out=tmpA[:, 1:W], in0=ext[:, j, 1:W], in1=ext[:, j, 0:W - 1], op=ALU.add),
                deps=chain_in + d_evs[-1:])
            S = tmpA
            if w >= 4:
                e = B.op(_CHAIN_ENG, lambda e_: e_.tensor_tensor(
                    out=tmpB[:, 3:W], in0=tmpA[:, 3:W], in1=tmpA[:, 1:W - 2], op=ALU.add), deps=[e])
                S = tmpB
            if w >= 8:
                e = B.op(_CHAIN_ENG, lambda e_: e_.tensor_tensor(
                    out=tmpA[:, 7:W], in0=tmpB[:, 7:W], in1=tmpB[:, 3:W - 4], op=ALU.add), deps=[e])
                S = tmpA
            if w >= 16:
                e = B.op(_CHAIN_ENG, lambda e_: e_.tensor_tensor(
                    out=tmpB[:, 15:W], in0=tmpA[:, 15:W], in1=tmpA[:, 7:W - 8], op=ALU.add), deps=[e])
                S = tmpB
            npr = 528 if is_c else TP
            ed = B.op("dve", lambda e_, S=S: e_.scalar_tensor_tensor(
                out=d_buf[:, j, 16:npr], in0=S[:, 16:npr], scalar=1.0 / w, in1=ext[:, j, 16:npr],
                op0=ALU.mult, op1=ALU.subtract), deps=[e])
            if is_c:
                Sv = S[:, 528:848].rearrange("p (q e) -> p q e", e=80)[:, :, 16:80]
                Ev = ext[:, j, 528:848].rearrange("p (q e) -> p q e", e=80)[:, :, 16:80]
                Dv = d_buf[:, j, 528:784].rearrange("p (q e) -> p q e", e=64)
                ed = B.op("dve", lambda e_, Sv=Sv, Ev=Ev, Dv=Dv: e_.scalar_tensor_tensor(
                    out=Dv, in0=Sv, scalar=1.0 / w, in1=Ev, op0=ALU.mult, op1=ALU.subtract), deps=[e])
            if is_a:
                e3 = B.op("dve", lambda e_, S=S: e_.tensor_tensor(
                    out=tmp16[:], in0=S[:, 16:32], in1=invcnt[:, g, :], op=ALU.mult), deps=[e, ev_cst])
                ed = B.op("dve", lambda e_: e_.tensor_tensor(
                    out=d_buf[:, j, 16:32], in0=tmp16[:], in1=ext[:, j, 16:32], op=ALU.subtract), deps=[e3, ed])
            d_evs.append(ed)
        u_evs = []
        for j in range(8):
            ib, slot, rdy = ring_acquire()
            for si, (c0, n) in enumerate(SUB):
                bi, ps, bf_ = get_bank()
                for k in range(KC):
                    dk = ([rdy] + (list(bf_) if bf_ else [])) if k == 0 else []
                    if j == 0:
                        dk.append(hm_ev[(si, k)])
                    ev = B.op("pe", lambda e, k=k: e.matmul(
                        ps[:, 0:n], lhsT=slot[:, k, :], rhs=h[:, k, c0:c0 + n],
                        start=(k == 0), stop=(k == KC - 1)),
                        deps=dk, sig=(k == KC - 1))
                ev_u = B.op("act", lambda e: e.activation(
                    out=u_buf[:, j, c0:c0 + n], in_=ps[:, 0:n], func=AF.Gelu), deps=[ev] + list(war))
                free_bank(bi, ev_u)
                u_evs.append(ev_u)
            ring_release(ib, ev)
        a_evs = []
        b_evs = []
        v_free = [None, None, None]
        vnb_free = [None, None, None]
        vnf_free = [None, None]
        bout_war = d_evs + save_evs
        pendq = []
        vstate = {"tcount": 0, "last_pe": None}

        def emit_v(hg, ti, blks):
            tc0 = 16 + 128 * ti
            sample = is_c and ti >= 4
            bsel = vstate["tcount"] % 3
            vstate["tcount"] += 1
            fsel = ti % 2
            bi, ps, bf_ = get_bank()
            first = True
            for c, (ib, slot, rdy) in enumerate(blks):
                for k in range(KC):
                    deps = []
                    if k == 0 and ti == 0:
                        deps.append(rdy)
                    if first:
                        deps += hm_evs + (list(bf_) if bf_ else [])
                        first = False
                    ev = B.op("pe", lambda e: e.matmul(
                        ps[:, c * 128:(c + 1) * 128], lhsT=h[:, k, tc0:tc0 + 128], rhs=slot[:, k, :],
                        start=(k == 0), stop=(k == KC - 1)),
                        deps=deps, sig=(c == 3 and k == KC - 1))
            vstate["last_pe"] = ev
            ev_v = B.op("act", lambda e: e.activation(
                out=v_sb[bsel][:], in_=ps[:, :], func=AF.Gelu), deps=[ev, v_free[bsel]])
            free_bank(bi, ev_v)
            for hh in range(4):
                e_sq = B.op("act", lambda e: e.activation(
                    out=scr[:, hh * 128:(hh + 1) * 128], in_=v_sb[bsel][:, hh * 128:(hh + 1) * 128],
                    func=AF.Square, accum_out=ss4[:, hh:hh + 1]), deps=[ev_v])
            e_rt = B.op("act", lambda e: e.activation(
                out=r4[:], in_=ss4[:], func=AF.Sqrt, bias=epsb[:, 0:1], scale=1.0 / 128),
                deps=[e_sq, ev_eps, state.get("r4_free")])
            e_rc = B.op("dve", lambda e: e.reciprocal(out=r4[:], in_=r4[:]), deps=[e_rt])
            nb0 = hg * 512
            if sample:
                for hh in range(4):
                    e7 = B.op("dve", lambda e: e.scalar_tensor_tensor(
                        out=vn_f32[fsel][:, hh * 128:(hh + 1) * 128],
                        in0=v_sb[bsel][:, hh * 128:(hh + 1) * 128], scalar=r4[:, hh:hh + 1],
                        in1=norm_bc[:, nb0 + hh * 128:nb0 + (hh + 1) * 128], op0=ALU.mult, op1=ALU.mult),
                        deps=[e_rc, vnf_free[fsel], ev_setup])
                row0 = (ti - 4) * 128
                vnf_free[fsel] = B.dma("sp", sv_sample_o[row0:row0 + 128, hg * 512:(hg + 1) * 512],
                                       vn_f32[fsel][:], s_sv[fsel], deps=[e7])
            for hh in range(4):
                e8 = B.op("dve", lambda e: e.scalar_tensor_tensor(
                    out=vn_bf[bsel][:, hh * 128:(hh + 1) * 128],
                    in0=v_sb[bsel][:, hh * 128:(hh + 1) * 128], scalar=r4[:, hh:hh + 1],
                    in1=norm_bc[:, nb0 + hh * 128:nb0 + (hh + 1) * 128], op0=ALU.mult, op1=ALU.mult),
                    deps=[e_rc, vnb_free[bsel], ev_setup])
            v_free[bsel] = e8
            state["r4_free"] = e8
            pendq.append((hg, bsel, e8, sample, tc0))

        def emit_sgu():
            hg, pb, pev, psample, ptc0 = pendq.pop(0)
            var = 1 if psample else 0
            bi2, ps2, bf2 = get_bank()
            for hh in range(4):
                hd = hg * 4 + hh
                deps = [pev, ev_wm, ev_blo, ev_setup_g]
                if hh == 0:
                    deps += (list(bf2) if bf2 else [])
                o = ps2[:, hh * 128:(hh + 1) * 128]
                B.op("pe", lambda e: e.matmul(
                    o, lhsT=vn_bf[pb][:, hh * 128:(hh + 1) * 128], rhs=wmT[:, var, hd, :],
                    start=True, stop=False), deps=deps, sig=False)
                B.op("pe", lambda e: e.matmul(
                    o, lhsT=sel[:, hd, :], rhs=b_sb[:, var, 0, :], start=False, stop=False), sig=False)
                ev_s = B.op("pe", lambda e: e.matmul(
                    o, lhsT=sel[:, hd, :], rhs=b_sb[:, var, 1, :], start=False, stop=True),
                    sig=(hh == 3))
            vnb_free[pb] = ev_s
            uv = u_buf[:, hg * 4:(hg + 1) * 4, ptc0:ptc0 + 128]
            ev_a = B.op("dve", lambda e: e.tensor_tensor(
                out=uv, in0=ps2[:, :].rearrange("p (h t) -> p h t", t=128), in1=uv, op=ALU.mult),
                deps=[ev_s] + u_evs)
            free_bank(bi2, ev_a)
            a_evs.append(ev_a)

        def emit_poolw(groups):
            for (g, dd, c0, n) in groups:
                bi, ps, bf_ = get_bank()
                for cc in range(2):
                    ev = B.op("pe", lambda e: e.matmul(
                        ps[:, 0:n], lhsT=poolw[:, g, cc, dd * 128:(dd + 1) * 128],
                        rhs=d_buf[:, 2 * g + cc, c0:c0 + n], start=(cc == 0), stop=(cc == 1)),
                        deps=(d_evs + [ev_setup_g] + (list(bf_) if bf_ else [])) if cc == 0 else (),
                        sig=(cc == 1))
                jj = 2 * g + dd
                ev_b = B.op("act", lambda e: e.activation(
                    out=b_out[:, jj, c0:c0 + n], in_=ps[:, 0:n], func=AF.Copy,
                    scale=pool_scale[:, jj:jj + 1]), deps=[ev, ev_cst] + bout_war)
                free_bank(bi, ev_b)
                b_evs.append(ev_b)

        pw_groups = [(g, dd, c0, n) for g in range(4) for dd in range(2) for (c0, n) in SUB]
        if _V_CROSS:
            vsteps = [(hg, ti) for hg in range(2) for ti in range(6)]
            blks = None
            for idx in range(len(vsteps) + 2):
                if idx < len(vsteps):
                    hg, ti = vsteps[idx]
                    if ti == 0:
                        blks = [ring_acquire() for _ in range(4)]
                    emit_v(hg, ti, blks)
                    if ti == 5:
                        for (ib, slot, rdy) in blks:
                            ring_release(ib, vstate["last_pe"])
                else:
                    half = len(pw_groups) // 2
                    emit_poolw(pw_groups[:half] if idx == len(vsteps) else pw_groups[half:])
                if idx >= 2:
                    emit_sgu()
        else:
            for hg in range(2):
                blks = [ring_acquire() for _ in range(4)]
                for ti in range(8):
                    if ti < 6:
                        emit_v(hg, ti, blks)
                        if ti == 5:
                            for (ib, slot, rdy) in blks:
                                ring_release(ib, vstate["last_pe"])
                    if ti >= 2:
                        emit_sgu()
            emit_poolw(pw_groups)
        assert not pendq
        x_out = []
        acc = StatsAcc(SUB)
        for dch in range(KC):
            ib, slot, rdy = ring_acquire()
            for si, (c0, n) in enumerate(SUB):
                bi, ps, bf_ = get_bank()
                for m in range(16):
                    rhs = u_buf[:, m, c0:c0 + n] if m < 8 else b_out[:, m - 8, c0:c0 + n]
                    deps = []
                    if m == 0:
                        deps = [rdy] + a_evs + (list(bf_) if bf_ else [])
                    if m == 8:
                        deps = b_evs
                    ev = B.op("pe", lambda e, m=m, rhs=rhs: e.matmul(
                        ps[:, 0:n], lhsT=slot[:, m, :], rhs=rhs, start=(m == 0), stop=(m == 15)),
                        deps=deps, sig=(m == 15))
                acc.flush()
                ev_x = B.op("dve", lambda e: e.tensor_tensor(
                    out=x[:, dch, c0:c0 + n], in0=ps[:, 0:n], in1=x[:, dch, c0:c0 + n], op=ALU.add),
                    deps=[ev, x_evs])
                free_bank(bi, ev_x)
                x_out.append(ev_x)
                acc.add(si, dch, ev_x)
            ring_release(ib, ev)
        acc.flush(keep_last=True)
        state["inter_war"] = [ev]
        return x_out, acc

    y_store_prev = None
    for p in range(npass):
        if p > 0:
            x_loaded = x_loaded_next

        def x_evs(k, x_loaded=x_loaded):
            return list(x_loaded) if k == "all" else [x_loaded[k // 4]]

        if y_store_prev is not None:
            state["inter_war"] = list(state["inter_war"]) + [y_store_prev]
        subs1 = SUBH if p == 0 else SUB
        x1, acc1 = ffn(0, 0, subs1, x_evs)
        x2, acc2 = mixer(p, subs1, x1, acc1)
        x3, acc3 = ffn(1, 2, SUB, x2, acc2,
                       after_gateup=(lambda ev, p=p: prefetch_x(p + 1, ev)) if p + 1 < npass else None)
        y_ev = rmsnorm_fm(3, SUB, x3, lambda k, c0, n: ystage[:, k, c0 - 16:c0 - 16 + n],
                          extra_deps=state["inter_war"], k_major=True, acc=acc3)
        for q in range(4):
            x_war[q] = [y_ev[(si, k)] for si in range(2) for k in range(4 * q, 4 * q + 4)]
        for q in (1, 0):
            y_store_prev = B.dma("sp", y_fm[p, :, 8 * q:8 * q + 8, :], ystage[:, 8 * q:8 * q + 8, :], s_y[p],
                                 deps=[y_ev[(si, k)] for si in range(2) for k in range(8 * q, 8 * q + 8)])
            if q == 1 and p + 1 < npass:
                x_loaded_next = load_x(p + 1)
    B.wait("sp", [(s_y[p], s_y[p].total) for p in range(npass)] +
           [(s_out, s_out.total), (s_sv[0], s_sv[0].total), (s_sv[1], s_sv[1].total)])
    assert ring_state["acquired"] == len(wq), (ring_state, len(wq))
    return nc


_PROGRAM = None


def _fm(tok):
    n, f = tok.shape
    return np.ascontiguousarray(tok.reshape(n, f // 128, 128).transpose(2, 1, 0))


def _tm(fm):
    q, k, n = fm.shape
    return fm.transpose(2, 1, 0).reshape(n, k * 128)


def kernel(x_prompt, x_sample, cache_pool, ffn1_norm, ffn1_w_gate, ffn1_w_up, ffn1_w_down, mix_norm,
           w_in, sgu_norm, sgu_w, sgu_b, pool_w, pool_scale, w_out, ffn2_norm, ffn2_w_gate, ffn2_w_up,
           ffn2_w_down, final_norm):
    global _PROGRAM
    if _PROGRAM is None:
        _PROGRAM = build_program()
    nc = _PROGRAM
    in_maps = _prep(x_prompt, x_sample, cache_pool, ffn1_norm, ffn1_w_gate, ffn1_w_up, ffn1_w_down, mix_norm,
                    w_in, sgu_norm, sgu_w, sgu_b, pool_w, pool_scale, w_out, ffn2_norm, ffn2_w_gate, ffn2_w_up,
                    ffn2_w_down, final_norm)
    res = run_bass_kernel_spmd(nc, in_maps, core_ids=list(range(NCORES)))
    return _post(res.results)


def _prep(x_prompt, x_sample, cache_pool, ffn1_norm, ffn1_w_gate, ffn1_w_up, ffn1_w_down, mix_norm,
          w_in, sgu_norm, sgu_w, sgu_b, pool_w, pool_scale, w_out, ffn2_norm, ffn2_w_gate, ffn2_w_up,
          ffn2_w_down, final_norm):
    f32 = np.float32
    a = lambda t: np.asarray(t, dtype=f32)
    x_prompt, x_sample, cache_pool = a(x_prompt), a(x_sample), a(cache_pool)

    shared = {
        "ffn1_w_gate": np.ascontiguousarray(a(ffn1_w_gate)[0]),
        "ffn1_w_up": np.ascontiguousarray(a(ffn1_w_up)[0]),
        "ffn1_w_down": np.ascontiguousarray(a(ffn1_w_down)[0]),
        "ffn2_w_gate": np.ascontiguousarray(a(ffn2_w_gate)[0]),
        "ffn2_w_up": np.ascontiguousarray(a(ffn2_w_up)[0]),
        "ffn2_w_down": np.ascontiguousarray(a(ffn2_w_down)[0]),
        "w_in": np.ascontiguousarray(a(w_in)[0]),
        "w_out": np.ascontiguousarray(a(w_out)[0]),
        "sgu_norm": np.ascontiguousarray(a(sgu_norm)[0].reshape(1, 1024)),
        "pool_w": np.ascontiguousarray(a(pool_w)[0]),
    }
    sw = a(sgu_w)[0]
    wT2 = np.zeros((128, 2, 8, 128), f32)
    wT2[:, 0] = sw.transpose(2, 0, 1)
    blk = sw[:, :64, :64].transpose(2, 0, 1)
    wT2[0:64, 1, :, 0:64] = blk
    wT2[64:128, 1, :, 64:128] = blk
    s_i = np.arange(128)[:, None]
    t_i = np.arange(128)[None, :]
    mask2 = np.zeros((128, 2, 128), f32)
    mask2[:, 0] = (s_i <= t_i)
    mask2[:, 1] = ((s_i // 64) == (t_i // 64)) & ((s_i % 64) <= (t_i % 64))
    sb_ = a(sgu_b)[0]
    b2 = np.zeros((128, 2, 128), f32)
    b2[0:8, 0] = sb_
    b2[0:8, 1, 0:64] = sb_[:, :64]
    b2[0:8, 1, 64:128] = sb_[:, :64]
    sel = np.zeros((128, 8, 128), f32)
    for hh in range(8):
        sel[hh, hh, :] = 1.0
    shared.update({"sgu_wT2": wT2, "sgu_mask2": mask2, "sgu_b2": b2, "sel": sel})
    gains = np.concatenate([a(g).reshape(16, 128).T for g in
                            (a(ffn1_norm)[0], a(mix_norm)[0], a(ffn2_norm)[0], a(final_norm))], axis=1)
    pscale = a(pool_scale)[0].reshape(8, 128).T

    in_maps = []
    for c in range(NCORES):
        b, hf = c // 2, c % 2
        base = hf * 2048
        xf = np.zeros((NPASS, 128, KC, TP), f32)
        tokA = np.zeros((TP, D), f32)
        if hf == 1:
            tokA[0:16] = x_prompt[b, base - 16:base]
        tokA[16:] = x_prompt[b, base:base + 768]
        xf[0] = _fm(tokA)
        xf[1, :, :, 16:] = _fm(x_prompt[b, base + 768:base + 1536])
        tokC = np.concatenate([x_prompt[b, base + 1536:base + 2048],
                               x_sample[4 * c:4 * c + 4].reshape(256, D)], axis=0)
        xf[2, :, :, 16:] = _fm(tokC)
        pref = np.zeros((128, 8, 4, 16), f32)
        for q in range(4):
            rows = np.zeros((16, 1024), f32)
            rows[1:] = cache_pool[0, 4 * c + q]
            pref[:, :, q, :] = _fm(rows)
        inv = np.zeros((128, 4, 16), f32)
        for g, w in enumerate((2, 4, 8, 16)):
            if hf == 0:
                inv[:, g, :] = 1.0 / np.minimum(w, np.arange(16) + 1)
            else:
                inv[:, g, :] = 1.0 / w
        cst = np.concatenate([gains, pscale, inv.reshape(128, 64)], axis=1).astype(f32)
        m = dict(shared)
        m.update({"x_fm": xf, "pool_pref": pref, "cst": np.ascontiguousarray(cst)})
        in_maps.append(m)
    return in_maps


def _post(results):
    f32 = np.float32
    y_prompt = np.zeros((4, 4096, D), f32)
    y_sample = np.zeros((32, 64, D), f32)
    sp_p = np.zeros((1, 4, 15, 1024), f32)
    sp_s = np.zeros((1, 32, 15, 1024), f32)
    sv_s = np.zeros((1, 32, 64, 1024), f32)
    for c in range(NCORES):
        r = results[c]
        if r is None:
            continue
        b, hf = c // 2, c % 2
        base = hf * 2048
        yf = r["y_fm"]
        y_prompt[b, base:base + 768] = _tm(yf[0])
        y_prompt[b, base + 768:base + 1536] = _tm(yf[1])
        tc = _tm(yf[2])
        y_prompt[b, base + 1536:base + 2048] = tc[:512]
        y_sample[4 * c:4 * c + 4] = tc[512:].reshape(4, 64, D)
        if hf == 1:
            sp_p[0, b] = _tm(r["sp_prompt"])[1:]
        for q in range(4):
            sp_s[0, 4 * c + q] = _tm(r["sp_sample"][:, :, q, :])[1:]
        sv_s[0, 4 * c:4 * c + 4] = r["sv_sample"].reshape(4, 64, 1024)
    return (y_prompt, y_sample, sp_p, sp_s, sv_s)
```

```python
import numpy as np
import concourse.bass as bass
import concourse.mybir as mybir
from concourse.bass_utils import run_bass_kernel_spmd

F32 = mybir.dt.float32
BF16 = mybir.dt.bfloat16
AF = mybir.ActivationFunctionType
ALU = mybir.AluOpType
AX = mybir.AxisListType

D = 2048
DFF = 5632
KC = 16
FC = 44
T = 768
HALO = 16
TP = T + HALO
EW = 848
NPASS = 3
NS = 6
EPS = 1e-6
NCORES = 8
_ACCUM = True
_CHAIN_ENG = "dve"
_V_CROSS = False
SUB = [(16, 384), (400, 384)]
SUBH = [(0, 400), (400, 384)]


class _Sem:
    _n = 0

    def __init__(self, nc, name):
        self.h = nc.alloc_semaphore(name)
        self.id = _Sem._n
        _Sem._n += 1
        self.total = 0


class _Eng:
    def __init__(self, nc, name, eng):
        self.eng = eng
        self.sem = _Sem(nc, "e_" + name)
        self.count = 0
        self.waited = {}


class Builder:
    def __init__(self):
        self.nc = bass.Bass("TRN2", target_bir_lowering=False)
        nc = self.nc
        self.E = {
            "pe": _Eng(nc, "pe", nc.tensor),
            "act": _Eng(nc, "act", nc.scalar),
            "dve": _Eng(nc, "dve", nc.vector),
            "pool": _Eng(nc, "pool", nc.gpsimd),
            "sp": _Eng(nc, "sp", nc.sync),
        }

    def wait(self, en, deps):
        E = self.E[en]
        best = {}
        for ev in deps:
            if ev is None:
                continue
            if isinstance(ev, list):
                for e2 in ev:
                    if e2 is not None:
                        s, v = e2
                        if best.get(s.id, (None, 0))[1] < v:
                            best[s.id] = (s, v)
                continue
            s, v = ev
            if best.get(s.id, (None, 0))[1] < v:
                best[s.id] = (s, v)
        for sid, (s, v) in best.items():
            if E.waited.get(sid, 0) < v:
                E.eng.wait_ge(s.h, v)
                E.waited[sid] = v

    def op(self, en, fn, deps=(), sig=True):
        self.wait(en, deps)
        E = self.E[en]
        ins = fn(E.eng)
        if sig:
            E.count += 1
            ins.then_inc(E.sem.h, 1)
            return (E.sem, E.count)
        return None

    def last_event(self, en):
        E = self.E[en]
        return (E.sem, E.count) if E.count > 0 else None

    def dma(self, en, out, in_, sem, deps=()):
        self.wait(en, deps)
        E = self.E[en]
        E.eng.dma_start(out=out, in_=in_).then_inc(sem.h, 16)
        sem.total += 16
        return (sem, sem.total)


def build_program(npass=NPASS):
    B = Builder()
    nc = B.nc

    def din(name, shape):
        return nc.dram_tensor(name, list(shape), F32, kind="ExternalInput").ap()

    def dout(name, shape):
        return nc.dram_tensor(name, list(shape), F32, kind="ExternalOutput").ap()

    x_fm = din("x_fm", [NPASS, 128, KC, TP])
    pool_pref = din("pool_pref", [128, 8, 4, 16])
    cst_d = din("cst", [128, 136])
    wT2_d = din("sgu_wT2", [128, 2, 8, 128])
    mask2_d = din("sgu_mask2", [128, 2, 128])
    b2_d = din("sgu_b2", [128, 2, 128])
    sel_d = din("sel", [128, 8, 128])
    snorm_d = din("sgu_norm", [1, 1024])
    poolw_d = din("pool_w", [4, 256, 256])
    Wg = [din("ffn1_w_gate", [D, DFF]), din("ffn2_w_gate", [D, DFF])]
    Wu = [din("ffn1_w_up", [D, DFF]), din("ffn2_w_up", [D, DFF])]
    Wd = [din("ffn1_w_down", [DFF, D]), din("ffn2_w_down", [DFF, D])]
    w_in = din("w_in", [D, 3072])
    w_out = din("w_out", [D, D])

    y_fm = dout("y_fm", [NPASS, 128, KC, T])
    sp_prompt_o = dout("sp_prompt", [128, 8, 16])
    sp_sample_o = dout("sp_sample", [128, 8, 4, 16])
    sv_sample_o = dout("sv_sample", [256, 1024])

    def sb(name, shape, dt):
        return nc.alloc_sbuf_tensor(name, list(shape), dt)

    x = sb("x", [128, KC, TP], F32)
    h = sb("h", [128, KC, TP], BF16)
    inter_t = sb("inter", [128, FC * TP], BF16)
    ring_slots = [sb(f"ring{i}", [128, KC, 128], BF16) for i in range(NS)]
    v_sb = [sb(f"v_sb{i}", [128, 512], F32) for i in range(3)]
    scr = sb("scr", [128, 512], BF16)
    vn_bf = [sb(f"vn_bf{i}", [128, 512], BF16) for i in range(3)]
    vn_f32 = [sb(f"vn_f32{i}", [128, 512], F32) for i in range(2)]
    rstd = sb("rstd", [128, TP], F32)
    sqtmp = [sb(f"sqtmp{i}", [128, 400], F32) for i in range(2)]
    silu_tmp = [sb(f"silu{i}", [128, 400], F32) for i in range(2)]
    cst = sb("cst_sb", [128, 136], F32)
    ones_f32 = sb("ones_f32", [128, 128], F32)
    wmT = sb("wmT", [128, 2, 8, 128], BF16)
    sel = sb("sel_sb", [128, 8, 128], BF16)
    b_sb = sb("b_sb", [128, 2, 2, 128], BF16)
    norm_bc = sb("norm_bc", [128, 1024], F32)
    poolw = sb("poolw", [128, 4, 2, 256], BF16)
    epsb = sb("epsb", [128, 1], F32)
    ss4 = sb("ss4", [128, 4], F32)
    r4 = sb("r4", [128, 4], F32)
    tmp16 = sb("tmp16", [128, 16], F32)
    xb_save = sb("xb_save", [128, 8, 16], F32)
    sp_prompt_sb = sb("sp_prompt_sb", [128, 8, 16], F32)
    sp_sample_sb = sb("sp_sample_sb", [128, 8, 4, 16], F32)

    inter = inter_t[:, :].rearrange("p (f t) -> p f t", t=TP)
    u_buf = inter_t[:, 0:8 * TP].rearrange("p (j t) -> p j t", t=TP)
    d_buf = inter_t[:, 8 * TP:16 * TP].rearrange("p (j t) -> p j t", t=TP)
    EXT0 = 16 * TP
    ext = inter_t[:, EXT0:EXT0 + 2 * 8 * EW].bitcast(F32).rearrange("p (j t) -> p j t", t=EW)
    TA0 = EXT0 + 2 * 8 * EW
    tmpA = inter_t[:, TA0:TA0 + 2 * EW].bitcast(F32)
    tmpB = inter_t[:, TA0 + 2 * EW:TA0 + 4 * EW].bitcast(F32)
    b_out = inter_t[:, EXT0:EXT0 + 8 * TP].rearrange("p (j t) -> p j t", t=TP)
    ystage = inter_t[:, 0:2 * KC * T].bitcast(F32).rearrange("p (k t) -> p k t", t=T)
    su_wT2 = inter_t[:, 0:4096].bitcast(F32).rearrange("p (v h t) -> p v h t", v=2, h=8)
    su_mask = inter_t[:, 4096:4096 + 512].bitcast(F32).rearrange("p (v t) -> p v t", v=2)
    su_b2 = inter_t[:, 4608:4608 + 512].bitcast(F32).rearrange("p (v t) -> p v t", v=2)
    su_bhi = inter_t[:, 5120:5120 + 512].bitcast(F32).rearrange("p (v t) -> p v t", v=2)

    hstage = h[:, :, :].rearrange("p k t -> p (k t)")[:, 0:2 * 8 * T].bitcast(F32).rearrange("p (k t) -> p k t", t=T)
    gains = cst[:, 0:64]
    pool_scale = cst[:, 64:72]
    invcnt = cst[:, 72:136].rearrange("p (g t) -> p g t", t=16)

    banks = [nc.alloc_psum_tensor(f"bank{i}", [128, 512], F32) for i in range(8)]
    bank_free = [None] * 8
    bank_ctr = [0]

    bank_reserved = set()

    def get_bank(reserve=False):
        while (bank_ctr[0] % 8) in bank_reserved:
            bank_ctr[0] += 1
        i = bank_ctr[0] % 8
        bank_ctr[0] += 1
        if reserve:
            bank_reserved.add(i)
        return i, banks[i], bank_free[i]

    def free_bank(i, evs):
        bank_free[i] = evs if isinstance(evs, list) else [evs]

    s_setup = _Sem(nc, "d_setup")
    s_setup_g = _Sem(nc, "d_setup_g")
    s_x = [[_Sem(nc, f"d_x{p}_{q}") for q in range(4)] for p in range(NPASS)]
    s_pref = _Sem(nc, "d_pref")
    s_xpre = [[_Sem(nc, f"d_xpre{p}_{q}") for q in range(2)] for p in range(NPASS)]
    s_out = _Sem(nc, "d_out")
    s_y = [_Sem(nc, f"d_y{p}") for p in range(NPASS)]
    s_sv = [_Sem(nc, "d_sv0"), _Sem(nc, "d_sv1")]
    ring_sem = [_Sem(nc, f"d_ring{i}") for i in range(NS)]

    wq = []
    ring_state = {"issued": 0, "acquired": 0, "released": 0}
    ring_last_use = [None] * NS

    def ring_issue():
        while ring_state["issued"] < min(len(wq), ring_state["released"] + NS):
            i = ring_state["issued"]
            s = i % NS
            src, nk = wq[i]
            B.dma("pool", ring_slots[s][:, 0:nk, :], src, ring_sem[s], deps=[ring_last_use[s]])
            ring_state["issued"] += 1

    def ring_acquire():
        i = ring_state["acquired"]
        ring_state["acquired"] += 1
        assert i < ring_state["issued"], "ring block not issued yet"
        s = i % NS
        return i, ring_slots[s], (ring_sem[s], 16 * (i // NS + 1))

    def ring_release(i, ev):
        assert i == ring_state["released"]
        ring_last_use[i % NS] = ev
        ring_state["released"] += 1
        ring_issue()

    def colblk(W, c0, r0=0, nk=KC):
        return (W[r0:r0 + nk * 128, c0:c0 + 128].rearrange("(k p) c -> p k c", p=128), nk)

    for p in range(npass):
        for wi in range(2):
            if wi == 1:
                for j in range(8):
                    wq.append(colblk(w_in, 2048 + j * 128))
                for j in range(8):
                    wq.append(colblk(w_in, j * 128))
                for j in range(8):
                    wq.append(colblk(w_in, 1024 + j * 128))
                for j in range(16):
                    wq.append(colblk(w_out, j * 128))
            for f in range(FC):
                wq.append(colblk(Wg[wi], f * 128))
                wq.append(colblk(Wu[wi], f * 128))
            for dch in range(KC):
                wq.append(colblk(Wd[wi], dch * 128, 0, 16))
                wq.append(colblk(Wd[wi], dch * 128, 2048, 16))
                wq.append(colblk(Wd[wi], dch * 128, 4096, 12))

    x_war = {q: None for q in range(4)}

    xpre = {}

    def prefetch_x(p, h_free_ev):
        xpre[p] = [B.dma("sp", hstage[:, 4 * q:4 * q + 4, :], x_fm[p, :, 4 * q:4 * q + 4, 16:TP],
                         s_xpre[p][q], deps=[h_free_ev]) for q in range(2)]

    def load_x(p):
        c_lo = 0 if p == 0 else 16
        evs = []
        for q in range(4):
            if p in xpre and q < 2:
                evs.append(B.op("act", lambda e: e.activation(
                    out=x[:, 4 * q:4 * q + 4, 16:TP], in_=hstage[:, 4 * q:4 * q + 4, :], func=AF.Copy),
                    deps=[x_war[q], xpre[p][q]]))
            else:
                evs.append(B.dma("sp", x[:, 4 * q:4 * q + 4, c_lo:TP], x_fm[p, :, 4 * q:4 * q + 4, c_lo:TP],
                                 s_x[p][q], deps=[x_war[q]]))
        if p in xpre:
            state["h_war"] = evs[0:2]
            state["sq_stage"] = list(xpre[p])
        return evs

    x_loaded = load_x(0)
    ev_cst = B.dma("sp", cst[:], cst_d, s_setup)
    B.dma("sp", su_wT2, wT2_d, s_setup)
    B.dma("sp", su_mask, mask2_d, s_setup)
    B.dma("sp", su_b2, b2_d, s_setup)
    ev_setup = B.dma("sp", norm_bc[:], bass.AP(snorm_d.tensor, 0, [[0, 128], [1, 1024]]), s_setup)
    ev_cst = ev_setup
    B.dma("pool", sel[:], sel_d, s_setup_g)
    ev_setup_g = B.dma("pool", poolw[:], poolw_d.rearrange("g (cc p) d -> p g cc d", p=128), s_setup_g)
    ring_issue()

    ev_ones = B.op("dve", lambda e: e.memset(ones_f32[:], 1.0 / D))
    ev_eps = B.op("dve", lambda e: e.memset(epsb[:], EPS))
    ev_wm = None
    for v in range(2):
        ev_wm = B.op("dve", lambda e, v=v: e.tensor_tensor(
            out=wmT[:, v, :, :], in0=su_wT2[:, v, :, :],
            in1=su_mask[:, v, :].unsqueeze(1).to_broadcast([128, 8, 128]), op=ALU.mult),
            deps=[ev_setup])
    ev_bhi = B.op("dve", lambda e: e.tensor_copy(out=b_sb[:, :, 0, :], in_=su_b2), deps=[ev_setup])
    ev_bhi2 = B.op("dve", lambda e: e.tensor_copy(out=su_bhi, in_=b_sb[:, :, 0, :]), deps=[ev_bhi])
    ev_blo = B.op("dve", lambda e: e.tensor_tensor(out=b_sb[:, :, 1, :], in0=su_b2, in1=su_bhi, op=ALU.subtract),
                  deps=[ev_bhi2])
    setup_done = [ev_ones, ev_eps, ev_wm, ev_blo, ev_setup, ev_setup_g, ev_cst]

    state = {"inter_war": [ev_blo, ev_wm]}

    sqst = {"free": [None, None], "cnt": 0}

    class StatsAcc:
        def __init__(self, subs):
            self.subs = subs
            self.n = [0] * len(subs)
            self.last = [None] * len(subs)
            self.pending = []

        def add(self, si, k, x_ev):
            c0, n = self.subs[si]
            if self.n[si] == 0:
                self.last[si] = B.op("act", lambda e: e.activation(
                    out=rstd[:, c0:c0 + n], in_=x[:, k, c0:c0 + n], func=AF.Square),
                    deps=[x_ev, state.get("rstd_readers")])
            else:
                b = sqst["cnt"] % 2
                sqst["cnt"] += 1
                assert all(pb != b for (_, pb, _) in self.pending)
                ev_sq = B.op("act", lambda e: e.activation(
                    out=sqtmp[b][:, 0:n], in_=x[:, k, c0:c0 + n], func=AF.Square),
                    deps=[x_ev, sqst["free"][b]])
                self.pending.append((si, b, ev_sq))
            self.n[si] += 1

        def flush(self, keep_last=False):
            while self.pending:
                if keep_last and len(self.pending) <= len(self.subs) and all(c == KC for c in self.n):
                    break
                si, b, ev_sq = self.pending.pop(0)
                c0, n = self.subs[si]
                ev = B.op("dve", lambda e: e.tensor_tensor(
                    out=rstd[:, c0:c0 + n], in0=rstd[:, c0:c0 + n], in1=sqtmp[b][:, 0:n], op=ALU.add),
                    deps=[ev_sq, self.last[si]])
                sqst["free"][b] = ev
                self.last[si] = ev

        def finalize(self, si):
            assert self.n[si] == KC
            c0, n = self.subs[si]
            tail = [pnd for pnd in self.pending if pnd[0] == si]
            self.pending = [pnd for pnd in self.pending if pnd[0] != si]
            bi, ps, bfree = get_bank()
            ev_mm = B.op("pe", lambda e: e.matmul(
                ps[:, 0:n], lhsT=ones_f32[:], rhs=rstd[:, c0:c0 + n], start=True, stop=(not tail)),
                deps=[self.last[si], ev_ones] + (list(bfree) if bfree else []))
            for i, (_, b, ev_sq) in enumerate(tail):
                ev_mm = B.op("pe", lambda e: e.matmul(
                    ps[:, 0:n], lhsT=ones_f32[:], rhs=sqtmp[b][:, 0:n], start=False, stop=(i == len(tail) - 1)),
                    deps=[ev_sq])
                sqst["free"][b] = ev_mm
            return bi, ps, ev_mm

    def xdep(x_evs, k):
        return x_evs(k) if callable(x_evs) else x_evs

    def rmsnorm_fm(gidx, subs, x_evs, out_fn, extra_deps=(), k_major=False, acc=None):
        out = {}
        pe_last = B.last_event("pe")
        rec = []
        assert acc is None or acc.subs == subs
        stage_sq = []
        for si, (c0, n) in enumerate(subs):
            if acc is not None:
                bi, ps, ev_mm = acc.finalize(si)
            else:
                bi, ps, bfree = get_bank()
                stage = state.get("sq_stage")
                for k in range(KC):
                    b = sqst["cnt"] % 2
                    sqst["cnt"] += 1
                    if stage is not None and k < 8 and si == 0:
                        src, sdep = hstage[:, k, c0 - 16:c0 - 16 + n], stage[k // 4]
                    else:
                        src, sdep = x[:, k, c0:c0 + n], xdep(x_evs, k)
                    ev_sq = B.op("act", lambda e: e.activation(
                        out=sqtmp[b][:, 0:n], in_=src, func=AF.Square),
                        deps=[sdep, sqst["free"][b]])
                    if stage is not None and k < 8 and si == 0:
                        stage_sq.append(ev_sq)
                    ev_mm = B.op("pe", lambda e: e.matmul(
                        ps[:, 0:n], lhsT=ones_f32[:], rhs=sqtmp[b][:, 0:n], start=(k == 0), stop=(k == KC - 1)),
                        deps=[ev_sq, ev_ones] + (list(bfree) if (k == 0 and bfree) else []))
                    sqst["free"][b] = ev_mm
                if si == len(subs) - 1:
                    state.pop("sq_stage", None)
            ev_s = B.op("act", lambda e: e.activation(
                out=rstd[:, c0:c0 + n], in_=ps[:, 0:n], func=AF.Sqrt, bias=epsb[:, 0:1], scale=1.0),
                deps=[ev_mm, ev_eps, state.get("rstd_readers")])
            free_bank(bi, ev_s)
            ev_r = B.op("dve", lambda e: e.reciprocal(out=rstd[:, c0:c0 + n], in_=rstd[:, c0:c0 + n]), deps=[ev_s])
            rec.append(ev_r)

            def emit_out(si, k, c0=c0, n=n, ev_r=ev_r):
                en = "dve"
                out[(si, k)] = B.op(en, lambda e: e.scalar_tensor_tensor(
                    out=out_fn(k, c0, n), in0=x[:, k, c0:c0 + n],
                    scalar=gains[:, gidx * 16 + k:gidx * 16 + k + 1], in1=rstd[:, c0:c0 + n],
                    op0=ALU.mult, op1=ALU.mult),
                    deps=[ev_r, xdep(x_evs, k), ev_cst, pe_last] + list(extra_deps) + stage_sq)
            if not k_major:
                for k in range(KC):
                    emit_out(si, k)
            else:
                rec[-1] = (emit_out, si)
        if k_major:
            for k in list(range(8, KC)) + list(range(8)):
                for (fn, si) in rec:
                    fn(si, k)
        state["rstd_readers"] = list(out.values())
        return out

    def ffn(wi, gidx, subs, x_evs, acc_in=None, after_gateup=None):
        h_ev = rmsnorm_fm(gidx, subs, x_evs, lambda k, c0, n: h[:, k, c0:c0 + n], acc=acc_in,
                          extra_deps=state.pop("h_war", []))
        x_all = x_evs("all") if callable(x_evs) else x_evs
        inter_evs = []
        silu_free = [None, None]
        step = 0
        first_write_deps = state["inter_war"]
        order = [(f, si) for f in range(FC) for si in range(len(subs))]
        if len(subs) == 2:
            order[0:4] = [(0, 0), (1, 0), (0, 1), (1, 1)]
        held = {}
        h_waited = set()
        last_pe = None
        for (f, si) in order:
            if f not in held:
                held[f] = (ring_acquire(), ring_acquire(), [0])
            (ig, slot_g, rdy_g), (iu, slot_u, rdy_u), nuse = held[f]
            first_use = (nuse[0] == 0)
            c0, n = subs[si]
            bg, psg, fg = get_bank()
            bu, psu, fu = get_bank()
            for k in range(KC):
                dk = (([rdy_g] if first_use else []) + (list(fg) if fg else [])) if k == 0 else []
                if si not in h_waited:
                    dk.append(h_ev[(si, k)])
                ev_g = B.op("pe", lambda e, k=k: e.matmul(
                    psg[:, 0:n], lhsT=slot_g[:, k, :], rhs=h[:, k, c0:c0 + n],
                    start=(k == 0), stop=(k == KC - 1)),
                    deps=dk, sig=(k == KC - 1))
            h_waited.add(si)
            for k in range(KC):
                ev_u = B.op("pe", lambda e, k=k: e.matmul(
                    psu[:, 0:n], lhsT=slot_u[:, k, :], rhs=h[:, k, c0:c0 + n],
                    start=(k == 0), stop=(k == KC - 1)),
                    deps=((([rdy_u] if first_use else []) + (list(fu) if fu else []))) if k == 0 else (),
                    sig=(k == KC - 1))
            last_pe = ev_u
            sbi = step % 2
            ev_a = B.op("act", lambda e, sbi=sbi: e.activation(
                out=silu_tmp[sbi][:, 0:n], in_=psg[:, 0:n], func=AF.Silu),
                deps=[ev_g, silu_free[sbi]])
            free_bank(bg, ev_a)
            ev_i = B.op("dve", lambda e, sbi=sbi: e.tensor_tensor(
                out=inter[:, f, c0:c0 + n], in0=silu_tmp[sbi][:, 0:n], in1=psu[:, 0:n], op=ALU.mult),
                deps=[ev_a, ev_u] + list(first_write_deps))
            first_write_deps = []
            free_bank(bu, ev_i)
            silu_free[sbi] = ev_i
            inter_evs.append(ev_i)
            step += 1
            nuse[0] += 1
            if nuse[0] == len(subs):
                ring_release(ig, last_pe)
                ring_release(iu, last_pe)
        inter_done = inter_evs[-len(subs):]
        if after_gateup is not None:
            after_gateup(last_pe)
        x_out = []
        acc = StatsAcc(subs)
        for dch in range(KC):
            blks = [ring_acquire() for _ in range(3)]
            last_pe = None
            for si, (c0, n) in enumerate(subs):
                bi, ps, bf_ = get_bank()
                cnt = 0
                for r, (ib, slot, rdy) in enumerate(blks):
                    nk = 16 if r < 2 else 12
                    for kk in range(nk):
                        f = r * 16 + kk
                        deps = []
                        if kk == 0:
                            deps.append(rdy)
                        if cnt == 0:
                            deps += inter_done + (list(bf_) if bf_ else [])
                        ev = B.op("pe", lambda e, slot=slot, kk=kk, f=f, cnt=cnt: e.matmul(
                            ps[:, 0:n], lhsT=slot[:, kk, :], rhs=inter[:, f, c0:c0 + n],
                            start=(cnt == 0), stop=(cnt == FC - 1)),
                            deps=deps, sig=(cnt == FC - 1))
                        cnt += 1
                last_pe = ev
                acc.flush()
                ev_x = B.op("dve", lambda e: e.scalar_tensor_tensor(
                    out=x[:, dch, c0:c0 + n], in0=ps[:, 0:n], scalar=0.5, in1=x[:, dch, c0:c0 + n],
                    op0=ALU.mult, op1=ALU.add), deps=[ev, x_all])
                free_bank(bi, ev_x)
                x_out.append(ev_x)
                acc.add(si, dch, ev_x)
            for (ib, slot, rdy) in blks:
                ring_release(ib, last_pe)
        acc.flush(keep_last=True)
        state["inter_war"] = [last_pe]
        return x_out, acc

    def mixer(p, subs_b, x_evs, acc_in=None):
        is_a = (p == 0)
        is_c = (p == 2)
        hm_ev = rmsnorm_fm(1, subs_b, x_evs, lambda k, c0, n: h[:, k, c0:c0 + n], acc=acc_in)
        hm_evs = list(hm_ev.values())
        war = state["inter_war"]
        pre_evs = []
        if p > 0:
            pre_evs.append(B.op("dve", lambda e: e.tensor_copy(out=ext[:, :, 0:16], in_=xb_save[:]),
                                deps=list(war) + hm_evs))
        if is_c:
            for q in range(4):
                ev_pf = B.dma("sp", ext[:, :, 528 + 80 * q:528 + 80 * q + 16], pool_pref[:, :, q, :], s_pref,
                              deps=list(war) + hm_evs)
            pre_evs.append(ev_pf)
        xb_evs = []
        xorder = [(j, si) for j in range(8) for si in range(len(subs_b))]
        if len(subs_b) == 2:
            xorder[0:4] = [(0, 0), (1, 0), (0, 1), (1, 1)]
        xheld = {}
        hm_waited = set()
        for (j, si) in xorder:
            if j not in xheld:
                xheld[j] = (ring_acquire(), [0])
            (ib, slot, rdy), nuse = xheld[j]
            c0, n = subs_b[si]
            bi, ps, bf_ = get_bank()
            for k in range(KC):
                dk = (([rdy] if nuse[0] == 0 else []) + (list(bf_) if bf_ else [])) if k == 0 else []
                if si not in hm_waited:
                    dk.append(hm_ev[(si, k)])
                ev = B.op("pe", lambda e, k=k: e.matmul(
                    ps[:, 0:n], lhsT=slot[:, k, :], rhs=h[:, k, c0:c0 + n],
                    start=(k == 0), stop=(k == KC - 1)),
                    deps=dk, sig=(k == KC - 1))
            hm_waited.add(si)
            if is_c and c0 == 400:
                ev1 = B.op("act", lambda e: e.activation(
                    out=ext[:, j, 400:528], in_=ps[:, 0:128], func=AF.Copy), deps=[ev] + list(war) + pre_evs)
                dst = ext[:, j, 528:848].rearrange("p (q e) -> p q e", e=80)[:, :, 16:80]
                src = ps[:, 128:384].rearrange("p (q e) -> p q e", e=64)
                ev2 = B.op("act", lambda e: e.activation(out=dst, in_=src, func=AF.Copy), deps=[ev])
                free_bank(bi, ev2)
                xb_evs += [ev1, ev2]
            else:
                ev1 = B.op("act", lambda e: e.activation(
                    out=ext[:, j, c0:c0 + n], in_=ps[:, 0:n], func=AF.Copy), deps=[ev] + list(war))
                free_bank(bi, ev1)
                xb_evs.append(ev1)
            nuse[0] += 1
            if nuse[0] == len(subs_b):
                ring_release(ib, ev)
        save_evs = []
        if not is_c:
            save_evs.append(B.op("dve", lambda e: e.tensor_copy(out=xb_save[:], in_=ext[:, :, 768:784]),
                                 deps=xb_evs + pre_evs))
        else:
            e1 = B.op("dve", lambda e: e.tensor_copy(out=sp_prompt_sb[:], in_=ext[:, :, 512:528]),
                      deps=xb_evs + pre_evs)
            src = ext[:, :, 528:848].rearrange("p j (q e) -> p j q e", e=80)[:, :, :, 64:80]
            e2 = B.op("dve", lambda e: e.tensor_copy(out=sp_sample_sb[:], in_=src), deps=xb_evs + pre_evs)
            B.dma("sp", sp_prompt_o, sp_prompt_sb[:], s_out, deps=[e1])
            B.dma("sp", sp_sample_o, sp_sample_sb[:], s_out, deps=[e2])
            save_evs += [e1, e2]
        W = EW if is_c else TP
        d_evs = []
        chain_in = xb_evs + pre_evs + list(war)
        for j in range(8):
            g = j // 2
            w = 2 << g
            e = B.op(_CHAIN_ENG, lambda e_: e_.tensor_tensor(
                out=tmpA[:, 1:W], in0=ext[:, j, 1:W], in1=ext[:, j, 0:W - 1], op=ALU.add),
                deps=chain_in + d_evs[-1:])
            S = tmpA
            if w >= 4:
                e = B.op(_CHAIN_ENG, lambda e_: e_.tensor_tensor(
                    out=tmpB[:, 3:W], in0=tmpA[:, 3:W], in1=tmpA[:, 1:W - 2], op=ALU.add), deps=[e])
                S = tmpB
            if w >= 8:
                e = B.op(_CHAIN_ENG, lambda e_: e_.tensor_tensor(
                    out=tmpA[:, 7:W], in0=tmpB[:, 7:W], in1=tmpB[:, 3:W - 4], op=ALU.add), deps=[e])
                S = tmpA
            if w >= 16:
                e = B.op(_CHAIN_ENG, lambda e_: e_.tensor_tensor(
                    out=tmpB[:, 15:W], in0=tmpA[:, 15:W], in1=tmpA[:, 7:W - 8], op=ALU.add), deps=[e])
                S = tmpB
            npr = 528 if is_c else TP
            ed = B.op("dve", lambda e_, S=S: e_.scalar_tensor_tensor(
                out=d_buf[:, j, 16:npr], in0=S[:, 16:npr], scalar=1.0 / w, in1=ext[:, j, 16:npr],
                op0=ALU.mult, op1=ALU.subtract), deps=[e])
            if is_c:
                Sv = S[:, 528:848].rearrange("p (q e) -> p q e", e=80)[:, :, 16:80]
                Ev = ext[:, j, 528:848].rearrange("p (q e) -> p q e", e=80)[:, :, 16:80]
                Dv = d_buf[:, j, 528:784].rearrange("p (q e) -> p q e", e=64)
                ed = B.op("dve", lambda e_, Sv=Sv, Ev=Ev, Dv=Dv: e_.scalar_tensor_tensor(
                    out=Dv, in0=Sv, scalar=1.0 / w, in1=Ev, op0=ALU.mult, op1=ALU.subtract), deps=[e])
            if is_a:
                e3 = B.op("dve", lambda e_, S=S: e_.tensor_tensor(
                    out=tmp16[:], in0=S[:, 16:32], in1=invcnt[:, g, :], op=ALU.mult), deps=[e, ev_cst])
                ed = B.op("dve", lambda e_: e_.tensor_tensor(
                    out=d_buf[:, j, 16:32], in0=tmp16[:], in1=ext[:, j, 16:32], op=ALU.subtract), deps=[e3, ed])
            d_evs.append(ed)
        u_evs = []
        for j in range(8):
            ib, slot, rdy = ring_acquire()
            for si, (c0, n) in enumerate(SUB):
                bi, ps, bf_ = get_bank()
                for k in range(KC):
                    dk = ([rdy] + (list(bf_) if bf_ else [])) if k == 0 else []
                    if j == 0:
                        dk.append(hm_ev[(si, k)])
                    ev = B.op("pe", lambda e, k=k: e.matmul(
                        ps[:, 0:n], lhsT=slot[:, k, :], rhs=h[:, k, c0:c0 + n],
                        start=(k == 0), stop=(k == KC - 1)),
                        deps=dk, sig=(k == KC - 1))
                ev_u = B.op("act", lambda e: e.activation(
                    out=u_buf[:, j, c0:c0 + n], in_=ps[:, 0:n], func=AF.Gelu), deps=[ev] + list(war))
                free_bank(bi, ev_u)
                u_evs.append(ev_u)
            ring_release(ib, ev)
        a_evs = []
        b_evs = []
        v_free = [None, None, None]
        vnb_free = [None, None, None]
        vnf_free = [None, None]
        bout_war = d_evs + save_evs
        pendq = []
        vstate = {"tcount": 0, "last_pe": None}

        def emit_v(hg, ti, blks):
            tc0 = 16 + 128 * ti
            sample = is_c and ti >= 4
            bsel = vstate["tcount"] % 3
            vstate["tcount"] += 1
            fsel = ti % 2
            bi, ps, bf_ = get_bank()
            first = True
            for c, (ib, slot, rdy) in enumerate(blks):
                for k in range(KC):
                    deps = []
                    if k == 0 and ti == 0:
                        deps.append(rdy)
                    if first:
                        deps += hm_evs + (list(bf_) if bf_ else [])
                        first = False
                    ev = B.op("pe", lambda e: e.matmul(
                        ps[:, c * 128:(c + 1) * 128], lhsT=h[:, k, tc0:tc0 + 128], rhs=slot[:, k, :],
                        start=(k == 0), stop=(k == KC - 1)),
                        deps=deps, sig=(c == 3 and k == KC - 1))
            vstate["last_pe"] = ev
            ev_v = B.op("act", lambda e: e.activation(
                out=v_sb[bsel][:], in_=ps[:, :], func=AF.Gelu), deps=[ev, v_free[bsel]])
            free_bank(bi, ev_v)
            for hh in range(4):
                e_sq = B.op("act", lambda e: e.activation(
                    out=scr[:, hh * 128:(hh + 1) * 128], in_=v_sb[bsel][:, hh * 128:(hh + 1) * 128],
                    func=AF.Square, accum_out=ss4[:, hh:hh + 1]), deps=[ev_v])
            e_rt = B.op("act", lambda e: e.activation(
                out=r4[:], in_=ss4[:], func=AF.Sqrt, bias=epsb[:, 0:1], scale=1.0 / 128),
                deps=[e_sq, ev_eps, state.get("r4_free")])
            e_rc = B.op("dve", lambda e: e.reciprocal(out=r4[:], in_=r4[:]), deps=[e_rt])
            nb0 = hg * 512
            if sample:
                for hh in range(4):
                    e7 = B.op("dve", lambda e: e.scalar_tensor_tensor(
                        out=vn_f32[fsel][:, hh * 128:(hh + 1) * 128],
                        in0=v_sb[bsel][:, hh * 128:(hh + 1) * 128], scalar=r4[:, hh:hh + 1],
                        in1=norm_bc[:, nb0 + hh * 128:nb0 + (hh + 1) * 128], op0=ALU.mult, op1=ALU.mult),
                        deps=[e_rc, vnf_free[fsel], ev_setup])
                row0 = (ti - 4) * 128
                vnf_free[fsel] = B.dma("sp", sv_sample_o[row0:row0 + 128, hg * 512:(hg + 1) * 512],
                                       vn_f32[fsel][:], s_sv[fsel], deps=[e7])
            for hh in range(4):
                e8 = B.op("dve", lambda e: e.scalar_tensor_tensor(
                    out=vn_bf[bsel][:, hh * 128:(hh + 1) * 128],
                    in0=v_sb[bsel][:, hh * 128:(hh + 1) * 128], scalar=r4[:, hh:hh + 1],
                    in1=norm_bc[:, nb0 + hh * 128:nb0 + (hh + 1) * 128], op0=ALU.mult, op1=ALU.mult),
                    deps=[e_rc, vnb_free[bsel], ev_setup])
            v_free[bsel] = e8
            state["r4_free"] = e8
            pendq.append((hg, bsel, e8, sample, tc0))

        def emit_sgu():
            hg, pb, pev, psample, ptc0 = pendq.pop(0)
            var = 1 if psample else 0
            bi2, ps2, bf2 = get_bank()
            for hh in range(4):
                hd = hg * 4 + hh
                deps = [pev, ev_wm, ev_blo, ev_setup_g]
                if hh == 0:
                    deps += (list(bf2) if bf2 else [])
                o = ps2[:, hh * 128:(hh + 1) * 128]
                B.op("pe", lambda e: e.matmul(
                    o, lhsT=vn_bf[pb][:, hh * 128:(hh + 1) * 128], rhs=wmT[:, var, hd, :],
                    start=True, stop=False), deps=deps, sig=False)
                B.op("pe", lambda e: e.matmul(
                    o, lhsT=sel[:, hd, :], rhs=b_sb[:, var, 0, :], start=False, stop=False), sig=False)
                ev_s = B.op("pe", lambda e: e.matmul(
                    o, lhsT=sel[:, hd, :], rhs=b_sb[:, var, 1, :], start=False, stop=True),
                    sig=(hh == 3))
            vnb_free[pb] = ev_s
            uv = u_buf[:, hg * 4:(hg + 1) * 4, ptc0:ptc0 + 128]
            ev_a = B.op("dve", lambda e: e.tensor_tensor(
                out=uv, in0=ps2[:, :].rearrange("p (h t) -> p h t", t=128), in1=uv, op=ALU.mult),
                deps=[ev_s] + u_evs)
            free_bank(bi2, ev_a)
            a_evs.append(ev_a)

        def emit_poolw(groups):
            for (g, dd, c0, n) in groups:
                bi, ps, bf_ = get_bank()
                for cc in range(2):
                    ev = B.op("pe", lambda e: e.matmul(
                        ps[:, 0:n], lhsT=poolw[:, g, cc, dd * 128:(dd + 1) * 128],
                        rhs=d_buf[:, 2 * g + cc, c0:c0 + n], start=(cc == 0), stop=(cc == 1)),
                        deps=(d_evs + [ev_setup_g] + (list(bf_) if bf_ else [])) if cc == 0 else (),
                        sig=(cc == 1))
                jj = 2 * g + dd
                ev_b = B.op("act", lambda e: e.activation(
                    out=b_out[:, jj, c0:c0 + n], in_=ps[:, 0:n], func=AF.Copy,
                    scale=pool_scale[:, jj:jj + 1]), deps=[ev, ev_cst] + bout_war)
                free_bank(bi, ev_b)
                b_evs.append(ev_b)

        pw_groups = [(g, dd, c0, n) for g in range(4) for dd in range(2) for (c0, n) in SUB]
        if _V_CROSS:
            vsteps = [(hg, ti) for hg in range(2) for ti in range(6)]
            blks = None
            for idx in range(len(vsteps) + 2):
                if idx < len(vsteps):
                    hg, ti = vsteps[idx]
                    if ti == 0:
                        blks = [ring_acquire() for _ in range(4)]
                    emit_v(hg, ti, blks)
                    if ti == 5:
                        for (ib, slot, rdy) in blks:
                            ring_release(ib, vstate["last_pe"])
                else:
                    half = len(pw_groups) // 2
                    emit_poolw(pw_groups[:half] if idx == len(vsteps) else pw_groups[half:])
                if idx >= 2:
                    emit_sgu()
        else:
            half = len(pw_groups) // 2
            for hg in range(2):
                blks = [ring_acquire() for _ in range(4)]
                for ti in range(8):
                    if ti < 6:
                        emit_v(hg, ti, blks)
                        if ti == 5:
                            for (ib, slot, rdy) in blks:
                                ring_release(ib, vstate["last_pe"])
                    elif hg == 1:
                        emit_poolw(pw_groups[:half] if ti == 6 else pw_groups[half:])
                    if ti >= 2:
                        emit_sgu()
        assert not pendq
        x_out = []
        acc = StatsAcc(SUB)
        for dch in range(KC):
            ib, slot, rdy = ring_acquire()
            for si, (c0, n) in enumerate(SUB):
                bi, ps, bf_ = get_bank()
                for m in range(16):
                    rhs = u_buf[:, m, c0:c0 + n] if m < 8 else b_out[:, m - 8, c0:c0 + n]
                    deps = []
                    if m == 0:
                        deps = [rdy] + a_evs + (list(bf_) if bf_ else [])
                    if m == 8:
                        deps = b_evs
                    ev = B.op("pe", lambda e, m=m, rhs=rhs: e.matmul(
                        ps[:, 0:n], lhsT=slot[:, m, :], rhs=rhs, start=(m == 0), stop=(m == 15)),
                        deps=deps, sig=(m == 15))
                acc.flush()
                ev_x = B.op("dve", lambda e: e.tensor_tensor(
                    out=x[:, dch, c0:c0 + n], in0=ps[:, 0:n], in1=x[:, dch, c0:c0 + n], op=ALU.add),
                    deps=[ev, x_evs])
                free_bank(bi, ev_x)
                x_out.append(ev_x)
                acc.add(si, dch, ev_x)
            ring_release(ib, ev)
        acc.flush(keep_last=True)
        state["inter_war"] = [ev]
        return x_out, acc

    y_store_prev = None
    for p in range(npass):
        if p > 0:
            x_loaded = x_loaded_next

        def x_evs(k, x_loaded=x_loaded):
            return list(x_loaded) if k == "all" else [x_loaded[k // 4]]

        if y_store_prev is not None:
            state["inter_war"] = list(state["inter_war"]) + [y_store_prev]
        subs1 = SUBH if p == 0 else SUB
        x1, acc1 = ffn(0, 0, subs1, x_evs)
        x2, acc2 = mixer(p, subs1, x1, acc1)
        x3, acc3 = ffn(1, 2, SUB, x2, acc2,
                       after_gateup=(lambda ev, p=p: prefetch_x(p + 1, ev)) if p + 1 < npass else None)
        y_ev = rmsnorm_fm(3, SUB, x3, lambda k, c0, n: ystage[:, k, c0 - 16:c0 - 16 + n],
                          extra_deps=state["inter_war"], k_major=True, acc=acc3)
        for q in range(4):
            x_war[q] = [y_ev[(si, k)] for si in range(2) for k in range(4 * q, 4 * q + 4)]
        for q in (1, 0):
            y_store_prev = B.dma("sp", y_fm[p, :, 8 * q:8 * q + 8, :], ystage[:, 8 * q:8 * q + 8, :], s_y[p],
                                 deps=[y_ev[(si, k)] for si in range(2) for k in range(8 * q, 8 * q + 8)])
            if q == 1 and p + 1 < npass:
                x_loaded_next = load_x(p + 1)
    B.wait("sp", [(s_y[p], s_y[p].total) for p in range(npass)] +
           [(s_out, s_out.total), (s_sv[0], s_sv[0].total), (s_sv[1], s_sv[1].total)])
    assert ring_state["acquired"] == len(wq), (ring_state, len(wq))
    return nc


_PROGRAM = None


def _fm(tok):
    n, f = tok.shape
    return np.ascontiguousarray(tok.reshape(n, f // 128, 128).transpose(2, 1, 0))


def _tm(fm):
    q, k, n = fm.shape
    return fm.transpose(2, 1, 0).reshape(n, k * 128)


def kernel(x_prompt, x_sample, cache_pool, ffn1_norm, ffn1_w_gate, ffn1_w_up, ffn1_w_down, mix_norm,
           w_in, sgu_norm, sgu_w, sgu_b, pool_w, pool_scale, w_out, ffn2_norm, ffn2_w_gate, ffn2_w_up,
           ffn2_w_down, final_norm):
    global _PROGRAM
    if _PROGRAM is None:
        _PROGRAM = build_program()
    nc = _PROGRAM
    in_maps = _prep(x_prompt, x_sample, cache_pool, ffn1_norm, ffn1_w_gate, ffn1_w_up, ffn1_w_down, mix_norm,
                    w_in, sgu_norm, sgu_w, sgu_b, pool_w, pool_scale, w_out, ffn2_norm, ffn2_w_gate, ffn2_w_up,
                    ffn2_w_down, final_norm)
    res = run_bass_kernel_spmd(nc, in_maps, core_ids=list(range(NCORES)))
    return _post(res.results)


def _prep(x_prompt, x_sample, cache_pool, ffn1_norm, ffn1_w_gate, ffn1_w_up, ffn1_w_down, mix_norm,
          w_in, sgu_norm, sgu_w, sgu_b, pool_w, pool_scale, w_out, ffn2_norm, ffn2_w_gate, ffn2_w_up,
          ffn2_w_down, final_norm):
    f32 = np.float32
    a = lambda t: np.asarray(t, dtype=f32)
    x_prompt, x_sample, cache_pool = a(x_prompt), a(x_sample), a(cache_pool)

    shared = {
        "ffn1_w_gate": np.ascontiguousarray(a(ffn1_w_gate)[0]),
        "ffn1_w_up": np.ascontiguousarray(a(ffn1_w_up)[0]),
        "ffn1_w_down": np.ascontiguousarray(a(ffn1_w_down)[0]),
        "ffn2_w_gate": np.ascontiguousarray(a(ffn2_w_gate)[0]),
        "ffn2_w_up": np.ascontiguousarray(a(ffn2_w_up)[0]),
        "ffn2_w_down": np.ascontiguousarray(a(ffn2_w_down)[0]),
        "w_in": np.ascontiguousarray(a(w_in)[0]),
        "w_out": np.ascontiguousarray(a(w_out)[0]),
        "sgu_norm": np.ascontiguousarray(a(sgu_norm)[0].reshape(1, 1024)),
        "pool_w": np.ascontiguousarray(a(pool_w)[0]),
    }
    sw = a(sgu_w)[0]
    wT2 = np.zeros((128, 2, 8, 128), f32)
    wT2[:, 0] = sw.transpose(2, 0, 1)
    blk = sw[:, :64, :64].transpose(2, 0, 1)
    wT2[0:64, 1, :, 0:64] = blk
    wT2[64:128, 1, :, 64:128] = blk
    s_i = np.arange(128)[:, None]
    t_i = np.arange(128)[None, :]
    mask2 = np.zeros((128, 2, 128), f32)
    mask2[:, 0] = (s_i <= t_i)
    mask2[:, 1] = ((s_i // 64) == (t_i // 64)) & ((s_i % 64) <= (t_i % 64))
    sb_ = a(sgu_b)[0]
    b2 = np.zeros((128, 2, 128), f32)
    b2[0:8, 0] = sb_
    b2[0:8, 1, 0:64] = sb_[:, :64]
    b2[0:8, 1, 64:128] = sb_[:, :64]
    sel = np.zeros((128, 8, 128), f32)
    for hh in range(8):
        sel[hh, hh, :] = 1.0
    shared.update({"sgu_wT2": wT2, "sgu_mask2": mask2, "sgu_b2": b2, "sel": sel})
    gains = np.concatenate([a(g).reshape(16, 128).T for g in
                            (a(ffn1_norm)[0], a(mix_norm)[0], a(ffn2_norm)[0], a(final_norm))], axis=1)
    pscale = a(pool_scale)[0].reshape(8, 128).T

    in_maps = []
    for c in range(NCORES):
        b, hf = c // 2, c % 2
        base = hf * 2048
        xf = np.zeros((NPASS, 128, KC, TP), f32)
        tokA = np.zeros((TP, D), f32)
        if hf == 1:
            tokA[0:16] = x_prompt[b, base - 16:base]
        tokA[16:] = x_prompt[b, base:base + 768]
        xf[0] = _fm(tokA)
        xf[1, :, :, 16:] = _fm(x_prompt[b, base + 768:base + 1536])
        tokC = np.concatenate([x_prompt[b, base + 1536:base + 2048],
                               x_sample[4 * c:4 * c + 4].reshape(256, D)], axis=0)
        xf[2, :, :, 16:] = _fm(tokC)
        pref = np.zeros((128, 8, 4, 16), f32)
        for q in range(4):
            rows = np.zeros((16, 1024), f32)
            rows[1:] = cache_pool[0, 4 * c + q]
            pref[:, :, q, :] = _fm(rows)
        inv = np.zeros((128, 4, 16), f32)
        for g, w in enumerate((2, 4, 8, 16)):
            if hf == 0:
                inv[:, g, :] = 1.0 / np.minimum(w, np.arange(16) + 1)
            else:
                inv[:, g, :] = 1.0 / w
        cst = np.concatenate([gains, pscale, inv.reshape(128, 64)], axis=1).astype(f32)
        m = dict(shared)
        m.update({"x_fm": xf, "pool_pref": pref, "cst": np.ascontiguousarray(cst)})
        in_maps.append(m)
    return in_maps


def _post(results):
    f32 = np.float32
    y_prompt = np.zeros((4, 4096, D), f32)
    y_sample = np.zeros((32, 64, D), f32)
    sp_p = np.zeros((1, 4, 15, 1024), f32)
    sp_s = np.zeros((1, 32, 15, 1024), f32)
    sv_s = np.zeros((1, 32, 64, 1024), f32)
    for c in range(NCORES):
        r = results[c]
        if r is None:
            continue
        b, hf = c // 2, c % 2
        base = hf * 2048
        yf = r["y_fm"]
        y_prompt[b, base:base + 768] = _tm(yf[0])
        y_prompt[b, base + 768:base + 1536] = _tm(yf[1])
        tc = _tm(yf[2])
        y_prompt[b, base + 1536:base + 2048] = tc[:512]
        y_sample[4 * c:4 * c + 4] = tc[512:].reshape(4, 64, D)
        if hf == 1:
            sp_p[0, b] = _tm(r["sp_prompt"])[1:]
        for q in range(4):
            sp_s[0, 4 * c + q] = _tm(r["sp_sample"][:, :, q, :])[1:]
        sv_s[0, 4 * c:4 * c + 4] = r["sv_sample"].reshape(4, 64, 1024)
    return (y_prompt, y_sample, sp_p, sp_s, sv_s)
```
